# Optimizing a Trainium2 kernel written in Bass

```python
import jax, jax.numpy as jnp
from jax import lax
import numpy as np

D_MODEL = 1024
BATCH = 2
SEQ = 16384
DEPTH = 1
DEC_BATCH = 16
DEC_SEQ = 2048
PAST_LEN = 128

CHUNK = 128
E_A = D_MODEL
A_GROUPS = 8
A_GROUP_W = E_A // A_GROUPS
E_B = D_MODEL
POOL_WINDOWS = (2, 4, 8, 16)
POOL_GROUPS = len(POOL_WINDOWS)
POOL_GROUP_W = E_B // POOL_GROUPS
IN_WIDTHS = (E_A, E_A, E_A, E_B, E_B, D_MODEL, D_MODEL)
IN_TOTAL = sum(IN_WIDTHS)
SPLIT_POINTS = [int(s) for s in np.cumsum(IN_WIDTHS)[:-1]]
DEEPNORM_ALPHA = (2.0 * DEPTH) ** 0.25
DEEPNORM_BETA = (8.0 * DEPTH) ** -0.25
LN_EPS = 1e-5

kernel_name = "gated_spatial_pool_hybrid_encoder"


def _layernorm(x, g, b):
    xf = x.astype(jnp.float32)
    mu = jnp.mean(xf, axis=-1, keepdims=True)
    xc = xf - mu
    var = jnp.mean(xc * xc, axis=-1, keepdims=True)
    return (xc * lax.rsqrt(var + LN_EPS) * g.astype(jnp.float32) + b.astype(jnp.float32)).astype(x.dtype)


def _centred_pool_minus_self(p, window):
    b, s, c = p.shape
    pf = p.astype(jnp.float32)
    csum = jnp.concatenate([jnp.zeros((b, 1, c), jnp.float32), jnp.cumsum(pf, axis=1)], axis=1)
    t = jnp.arange(s)
    lo = jnp.clip(t - window // 2, 0, s)
    hi = jnp.clip(t + window - window // 2, 0, s)
    sums = jnp.take(csum, hi, axis=1) - jnp.take(csum, lo, axis=1)
    cnt = (hi - lo).astype(jnp.float32)[None, :, None]
    return (sums / cnt - pf).astype(p.dtype)


def _layer(x, w_in, b_in, ln_v_g, ln_v_b, w_spatial, b_spatial, w_pool, b_pool, pool_scale,
           w_br_a, w_br_b, w_out, b_out, ln_g, ln_b):
    bsz, s, _ = x.shape
    h = jnp.einsum('bsd,de->bse', x, w_in) + b_in
    u, v, z_a, p, z_b, g_a, g_b = jnp.split(h, SPLIT_POINTS, axis=-1)

    v = _layernorm(v, ln_v_g, ln_v_b)
    v = v.reshape(bsz, s // CHUNK, CHUNK, A_GROUPS, A_GROUP_W)
    v = jnp.einsum('bnpgc,gqp->bnqgc', v, w_spatial) + b_spatial.T[None, None, :, :, None]
    br_a = u * v.reshape(bsz, s, E_A) * jax.nn.silu(z_a)

    p = p.reshape(bsz, s, POOL_GROUPS, POOL_GROUP_W)
    pooled = jnp.stack([_centred_pool_minus_self(p[:, :, i, :], w) for i, w in enumerate(POOL_WINDOWS)], axis=2)
    m = jnp.einsum('bsgc,gcd->bsgd', pooled, w_pool) + b_pool
    br_b = m.reshape(bsz, s, E_B) * pool_scale * jax.nn.silu(z_b)

    merged = (jax.nn.sigmoid(g_a) * jnp.einsum('bse,ed->bsd', br_a, w_br_a)
              + jax.nn.sigmoid(g_b) * jnp.einsum('bse,ed->bsd', br_b, w_br_b))
    out = jnp.einsum('bsd,de->bse', merged, w_out) + b_out
    return _layernorm(DEEPNORM_ALPHA * x + out, ln_g, ln_b)


def _trunk(x, w_in, b_in, ln_v_g, ln_v_b, w_spatial, b_spatial, w_pool, b_pool, pool_scale,
           w_br_a, w_br_b, w_out, b_out, ln_g, ln_b):
    for l in range(DEPTH):
        x = _layer(x, w_in[l], b_in[l], ln_v_g[l], ln_v_b[l], w_spatial[l], b_spatial[l], w_pool[l],
                   b_pool[l], pool_scale[l], w_br_a[l], w_br_b[l], w_out[l], b_out[l], ln_g[l], ln_b[l])
    return x


def setup_inputs(seed: int = 0) -> dict:
    key = jax.random.key(seed)
    ks = jax.random.split(key, 20)
    n = jax.random.normal
    f32 = jnp.float32
    return {
        "x_prompt": n(ks[0], (BATCH, SEQ, D_MODEL), f32),
        "x_sample": n(ks[1], (DEC_BATCH, DEC_SEQ, D_MODEL), f32),
        "w_in": n(ks[2], (DEPTH, D_MODEL, IN_TOTAL), f32) * D_MODEL ** -0.5,
        "b_in": n(ks[3], (DEPTH, IN_TOTAL), f32) * 0.02,
        "ln_v_g": 1.0 + 0.05 * n(ks[4], (DEPTH, E_A), f32),
        "ln_v_b": 0.02 * n(ks[5], (DEPTH, E_A), f32),
        "w_spatial": n(ks[6], (DEPTH, A_GROUPS, CHUNK, CHUNK), f32) * CHUNK ** -0.5,
        "b_spatial": 1.0 + 0.05 * n(ks[7], (DEPTH, A_GROUPS, CHUNK), f32),
        "w_pool": n(ks[8], (DEPTH, POOL_GROUPS, POOL_GROUP_W, POOL_GROUP_W), f32) * POOL_GROUP_W ** -0.5,
        "b_pool": 0.02 * n(ks[9], (DEPTH, POOL_GROUPS, POOL_GROUP_W), f32),
        "pool_scale": 1.0 + 0.05 * n(ks[10], (DEPTH, E_B), f32),
        "w_br_a": n(ks[11], (DEPTH, E_A, D_MODEL), f32) * (E_A ** -0.5 * DEEPNORM_BETA),
        "w_br_b": n(ks[12], (DEPTH, E_B, D_MODEL), f32) * (E_B ** -0.5 * DEEPNORM_BETA),
        "w_out": n(ks[13], (DEPTH, D_MODEL, D_MODEL), f32) * (D_MODEL ** -0.5 * DEEPNORM_BETA),
        "b_out": 0.02 * n(ks[14], (DEPTH, D_MODEL), f32),
        "ln_g": 1.0 + 0.05 * n(ks[15], (DEPTH, D_MODEL), f32),
        "ln_b": 0.02 * n(ks[16], (DEPTH, D_MODEL), f32),
    }


def reference(x_prompt, x_sample, w_in, b_in, ln_v_g, ln_v_b, w_spatial, b_spatial, w_pool, b_pool,
              pool_scale, w_br_a, w_br_b, w_out, b_out, ln_g, ln_b):
    y_prompt = _trunk(x_prompt, w_in, b_in, ln_v_g, ln_v_b, w_spatial, b_spatial, w_pool, b_pool,
                      pool_scale, w_br_a, w_br_b, w_out, b_out, ln_g, ln_b)
    y_sample = _trunk(x_sample, w_in, b_in, ln_v_g, ln_v_b, w_spatial, b_spatial, w_pool, b_pool,
                      pool_scale, w_br_a, w_br_b, w_out, b_out, ln_g, ln_b)
    return (y_prompt, y_sample)
```

```python
import numpy as np
import ml_dtypes
import concourse.bass as bass
import concourse.mybir as mybir
from concourse.bass_utils import run_bass_kernel_spmd

F32 = mybir.dt.float32
BF16 = mybir.dt.bfloat16
AF = mybir.ActivationFunctionType
ALU = mybir.AluOpType

D = 1024
NCORES = 8
T = 512
NCH = 4
SEGS = (4096, 2048, 2048)
SEG_PAD_OFF = (0, 4112, 6176)
SEG_OUT_OFF = (0, 4096, 6144)
XPAD_ROWS = 8240
NTOK = 8192
NUNITS = 20
NSLOT = 4
ALPHA = float(2.0 ** 0.25)
LN_EPS = 1e-5
POOL_WINDOWS = (2, 4, 8, 16)
CB_U, CB_V, CB_ZA, CB_P, CB_ZB, CB_GA, CB_GB = 0, 8, 16, 24, 32, 40, 48


class Sched:
    def __init__(self, nc):
        self.nc = nc
        self.eng = {"pe": nc.tensor, "act": nc.scalar, "dve": nc.vector,
                    "pool": nc.gpsimd, "sp": nc.sync}
        self.sem = {}
        self.tick = {}
        for e in ("pe", "act", "dve", "pool"):
            self.sem[e] = nc.alloc_semaphore("s_" + e)
            self.tick[e] = 0
        self.waited = {e: {} for e in self.eng}
        self.last_write = {}
        self.readers = {}
        self.dma_sem = {}
        self.dma_cnt = {}

    def _deps(self, reads, writes):
        deps = {}

        def add(k, v):
            if k not in deps or deps[k] < v:
                deps[k] = v

        for b in reads:
            lw = self.last_write.get(b)
            if lw is not None:
                add(*lw)
            if b.startswith("ps"):
                for k, v in self.readers.get(b, {}).items():
                    add(k, v)
        for b in writes:
            lw = self.last_write.get(b)
            if lw is not None:
                add(*lw)
            for k, v in self.readers.get(b, {}).items():
                add(k, v)
        return deps

    def _emit_waits(self, engine, deps, embed=False):
        need = []
        for k, v in deps.items():
            if k == "pe" and engine == "pe":
                continue
            if k in self.tick and v > self.tick[k]:
                raise RuntimeError(f"dep on unclosed milestone {k} {v} > {self.tick[k]}")
            if self.waited[engine].get(k, 0) >= v:
                continue
            sem = self.sem[k] if k in self.sem else self.dma_sem[k]
            need.append((sem, v))
            self.waited[engine][k] = v
        last = need.pop() if (embed and need) else None
        for sem, v in need:
            self.eng[engine].wait_ge(sem, v)
        return last

    def _record(self, key, val, reads, writes):
        for b in writes:
            self.last_write[b] = (key, val)
            self.readers[b] = {}
        for b in reads:
            r = self.readers.setdefault(b, {})
            if r.get(key, 0) < val:
                r[key] = val

    def op(self, engine, fn, reads=(), writes=(), inc=True):
        w = self._emit_waits(engine, self._deps(reads, writes), embed=True)
        ins = fn(self.eng[engine])
        if w is not None:
            ins._wait_ge(w[0], w[1])
        if inc:
            self.tick[engine] += 1
            ins.then_inc(self.sem[engine], 1)
            val = self.tick[engine]
        else:
            assert engine == "pe"
            val = self.tick[engine] + 1
        self._record(engine, val, reads, writes)
        return ins

    def prewait(self, engine, reads=(), writes=()):
        self._emit_waits(engine, self._deps(reads, writes))

    def dma(self, queue, out, in_, slot, reads=(), writes=(), transpose=False):
        w = self._emit_waits(queue, self._deps(reads, writes), embed=True)
        if slot not in self.dma_sem:
            self.dma_sem[slot] = self.nc.alloc_semaphore("d_" + slot)
            self.dma_cnt[slot] = 0
        if transpose:
            ins = self.eng[queue].dma_start_transpose(out=out, in_=in_)
        else:
            ins = self.eng[queue].dma_start(out=out, in_=in_)
        if w is not None:
            ins._wait_ge(w[0], w[1])
        self.dma_cnt[slot] += 16
        ins.then_inc(self.dma_sem[slot], 16)
        self._record(slot, self.dma_cnt[slot], reads, writes)
        return ins

    def wait_all_dma(self, engine, slots):
        for s in slots:
            if self.dma_cnt.get(s, 0) > 0 and self.waited[engine].get(s, 0) < self.dma_cnt[s]:
                self.eng[engine].wait_ge(self.dma_sem[s], self.dma_cnt[s])
                self.waited[engine][s] = self.dma_cnt[s]


def build_nc():
    nc = bass.Bass("TRN2", target_bir_lowering=False)
    S = Sched(nc)

    xpad = nc.dram_tensor("xpad", [XPAD_ROWS, D], F32, kind="ExternalInput").ap()
    wall = nc.dram_tensor("wall", [NUNITS, 128, 8, 512], F32, kind="ExternalInput").ap()
    ppar_d = nc.dram_tensor("ppar", [128, 72], F32, kind="ExternalInput").ap()
    tpar_d = nc.dram_tensor("tpar", [7, D], F32, kind="ExternalInput").ap()
    bsp_d = nc.dram_tensor("bsp", [1, D], F32, kind="ExternalInput").ap()
    wspT_d = nc.dram_tensor("wspT", [128, 8, 128], F32, kind="ExternalInput").ap()
    wpool_d = nc.dram_tensor("wpool", [128, 8, 256], F32, kind="ExternalInput").ap()
    pa1_d = nc.dram_tensor("pa1", [128, 40, 128], F32, kind="ExternalInput").ap()
    pa2_d = nc.dram_tensor("pa2", [128, 40, 16], F32, kind="ExternalInput").ap()
    y_d = nc.dram_tensor("y", [NTOK, D], F32, kind="ExternalOutput").ap()
    wall_bf = nc.dram_tensor("wall_bf", [NUNITS, 128, 8, 512], BF16, kind="Internal").ap()
    xbf = nc.dram_tensor("xbf", [2, 640, D], BF16, kind="Internal").ap()

    A = nc.alloc_sbuf_tensor
    pa1 = A("pa1_sb", [128, 40, 128], BF16)
    pa2 = A("pa2_sb", [128, 40, 16], BF16)
    ppar = A("ppar_sb", [128, 72], F32)
    tpar = A("tpar_sb", [128, 6, D], F32)
    wspT = A("wspT_sb", [128, 8, 128], BF16)
    wpool = A("wpool_sb", [128, 8, 256], BF16)
    bsp_rows = A("bsp_rows", [128, D], BF16)
    bv_rows = A("bv_rows", [128, D], BF16)
    ones33 = A("ones33", [128, 128], BF16)
    cst = A("cst", [128, 3], F32)
    ws = [A(f"ws{i}", [128, 8, 512], BF16) for i in range(NSLOT)]
    xTs = [A(f"xT{i}", [128, 8, 640], BF16) for i in range(2)]
    xf = [A(f"xf{i}", [128, D], F32) for i in range(4)]
    v32 = [A(f"v32_{i}", [128, D], F32) for i in range(2)]
    vfull = A("vfull", [128, NCH, D], BF16)
    pbuf = A("pbuf", [128, 5, D], BF16)
    sza = [A(f"sza{i}", [128, T], F32) for i in range(2)]
    szb = [A(f"szb{i}", [128, T], F32) for i in range(2)]
    us = A("us", [128, 8, T], F32)
    bra = A("bra", [128, 8, T], BF16)
    pooled = A("pooled", [128, 8, T], BF16)
    brb = A("brb", [128, 8, T], BF16)
    merged = A("merged", [128, 8, T], BF16)
    sga = [A(f"sga{i}", [128, T], F32) for i in range(2)]
    sgb = [A(f"sgb{i}", [128, T], F32) for i in range(2)]
    stats = A("stats", [128, 8, 2, 6], F32)
    mv = A("mv", [128, 8, 4], F32)
    ps = nc.alloc_psum_tensor("ps", [128, 8, 512], F32)
    bank_ctr = [0]

    def next_bank():
        b = bank_ctr[0] % 8
        bank_ctr[0] += 1
        return b

    XTN = [[f"xT{par}_{kc}" for kc in range(8)] for par in range(2)]

    def seg_pos(ti):
        seg, t = tiles[ti]
        nt = SEGS[seg] // T
        return t, (t == 0), (t == nt - 1)

    def x_rows(ti):
        return 528 if seg_pos(ti)[2] else 640

    def pslot(ti, win):
        return (4 * seg_pos(ti)[0] + win) % 5

    def cast_x(ti):
        seg, t = tiles[ti]
        base = SEG_PAD_OFF[seg] + T * t
        par = ti % 2
        R = x_rows(ti)
        S.dma("pool", xbf[par][0:R, :], xpad[base:base + R, :], f"xcast{par}", writes=[f"xbf{par}"])

    def transpose_x(ti):
        par = ti % 2
        R = x_rows(ti)
        for kc in range(8):
            S.dma("sp", xTs[par][:, kc, 0:R], xbf[par][0:R, kc * 128:(kc + 1) * 128], f"xtr{par}",
                  reads=[f"xbf{par}"], writes=[XTN[par][kc]], transpose=True)
        for kc in range(8):
            S.last_write[XTN[par][kc]] = (f"xtr{par}", S.dma_cnt[f"xtr{par}"])

    tiles = []
    for s, L in enumerate(SEGS):
        for t in range(L // T):
            tiles.append((s, t))
    cast_x(0)
    S.dma("sp", ppar[:], ppar_d, "c_ppar", writes=["ppar"])
    S.dma("sp", tpar[:].rearrange("p k d -> p (k d)"),
          tpar_d[1:7, :].rearrange("k d -> (k d)").partition_broadcast(128), "c_tpar", writes=["tpar"])
    transpose_x(0)

    def make_hilo_rows(src_row_ap, rows_tile, rows_name, sem, b32, n32, bhi, nhi, bhi32, nhi32):
        S.op("dve", lambda e: e.memset(b32, 0.0), writes=n32)
        S.dma("act", b32[0:1, :], src_row_ap, sem, reads=n32, writes=n32)
        S.dma("act", b32[32:33, :], src_row_ap, sem, writes=n32)
        S.op("dve", lambda e: e.memset(rows_tile[:], 0.0), writes=[rows_name])
        S.op("dve", lambda e: e.tensor_copy(out=bhi, in_=b32), reads=n32, writes=nhi)
        S.op("dve", lambda e: e.tensor_copy(out=bhi32, in_=bhi), reads=nhi, writes=nhi32)
        S.op("dve", lambda e: e.tensor_sub(out=b32, in0=b32, in1=bhi32), reads=n32 + nhi32, writes=n32)
        S.op("dve", lambda e: e.tensor_copy(out=rows_tile[0:1, :], in_=bhi[0:1, :]),
             reads=nhi + [rows_name], writes=[rows_name])
        S.op("dve", lambda e: e.tensor_copy(out=rows_tile[32:33, :], in_=b32[32:33, :]),
             reads=n32 + [rows_name], writes=[rows_name])

    make_hilo_rows(tpar_d[0:1, :], bv_rows, "bv_rows", "c_bv",
                   v32[0][0:33, :], ["v32_0h0", "v32_0h1"], pbuf[0:33, 0, :], ["p0h0", "p0h1"],
                   xf[0][0:33, :], ["xf0h0", "xf0h1"])
    make_hilo_rows(bsp_d, bsp_rows, "bsp_rows", "c_bsp",
                   v32[1][0:33, :], ["v32_1h0", "v32_1h1"], pbuf[0:33, 1, :], ["p1h0", "p1h1"],
                   xf[1][0:33, :], ["xf1h0", "xf1h1"])
    S.op("dve", lambda e: e.memset(ones33[:], 1.0), writes=["ones33"])
    S.op("dve", lambda e: e.memset(pbuf[:], 0.0), writes=[f"p{i}h{h}" for i in range(5) for h in range(2)])
    S.op("pool", lambda e: e.memset(cst[:, 0:1], LN_EPS), writes=["cst"])
    S.op("pool", lambda e: e.memset(cst[:, 1:2], -0.5), writes=["cst"])
    S.op("pool", lambda e: e.memset(cst[:, 2:3], ALPHA), writes=["cst"])

    wstate = {"next": 0}

    def ensure_loaded(upto):
        while wstate["next"] <= upto and wstate["next"] < NUNITS * len(tiles):
            U = wstate["next"]
            u = U % NUNITS
            sl = U % NSLOT
            if U < NUNITS:
                S.dma("pool", ws[sl][:], wall[u], f"wldc{sl}", writes=[f"ws{sl}"])
                S.dma("sp", wall_bf[u], ws[sl][:], f"wwb{u}", reads=[f"ws{sl}"], writes=[f"wall{u}"])
            else:
                S.dma("sp", ws[sl][:], wall_bf[u], f"wld{sl}", reads=[f"wall{u}"], writes=[f"ws{sl}"])
            wstate["next"] += 1

    def wslot(ti, u):
        U = ti * NUNITS + u
        return ws[U % NSLOT], f"ws{U % NSLOT}"

    deferred = []

    def run_deferred():
        while deferred:
            deferred.pop(0)()

    def chunk_step(part1, part2):
        part1()
        run_deferred()
        deferred.append(part2)

    def mm_group(b, lhs_fn, rhs_fn, nk, reads, out_ap=None):
        o = ps[:, b, :] if out_ap is None else out_ap
        for k in range(nk):
            S.op("pe", lambda e, k=k: e.matmul(o, lhsT=lhs_fn(k), rhs=rhs_fn(k), start=(k == 0), stop=(k == nk - 1)),
                 reads=reads, writes=[f"ps{b}"], inc=(k == nk - 1))

    def stage_v(ti):
        xT = xTs[ti % 2]
        XT_ALL = XTN[ti % 2]
        ensure_loaded(ti * NUNITS + 5)
        w0, n0 = wslot(ti, 2)
        w1, n1 = wslot(ti, 3)
        for j in range(NCH):
            sl = j
            vs = j % 2
            mvs = mv[:, sl, :]
            banks = []
            src = v32[vs]
            names = [f"v32_{vs}h0", f"v32_{vs}h1"]

            def part1(j=j, sl=sl, mvs=mvs, banks=banks, src=src, names=names):
                for h, (w, wn) in enumerate(((w0, n0), (w1, n1))):
                    b = next_bank()
                    banks.append(b)
                    o = ps[:, b, :]
                    S.op("pe", lambda e, h=h, o=o: e.matmul(o, lhsT=ones33[:], rhs=bv_rows[:, h * 512:(h + 1) * 512],
                                                            start=True, stop=False),
                         reads=["ones33", "bv_rows"], writes=[f"ps{b}"], inc=False)
                    for k in range(8):
                        S.op("pe", lambda e, k=k, w=w, o=o: e.matmul(
                            o, lhsT=xT[:, k, 8 + 128 * j:8 + 128 * j + 128], rhs=w[:, k, :], start=False, stop=(k == 7)),
                            reads=XT_ALL + [wn], writes=[f"ps{b}"], inc=(k == 7))
                    S.op("dve", lambda e, h=h, b=b: e.bn_stats(out=stats[:, sl, h, :], in_=ps[:, b, :]),
                         reads=[f"ps{b}"], writes=[f"stats{sl}"] if h == 1 else [f"stats{sl}a"])
                    S.op("act", lambda e, h=h, b=b: e.copy(out=src[:, h * 512:(h + 1) * 512], in_=ps[:, b, :]),
                         reads=[f"ps{b}"], writes=[names[h]])
                S.op("dve", lambda e: e.bn_aggr(out=mvs[:, 0:2], in_=stats[:, sl, :, :]),
                     reads=[f"stats{sl}", f"stats{sl}a"], writes=[f"mv{sl}"])
                S.op("pool", lambda e: e.tensor_tensor(out=mvs[:, 3:4], in0=mvs[:, 1:2], in1=cst[:, 0:1], op=ALU.add),
                     reads=[f"mv{sl}", "cst"], writes=[f"mvt{sl}"])
                S.op("pool", lambda e: e.tensor_tensor(out=mvs[:, 2:3], in0=mvs[:, 3:4], in1=cst[:, 1:2], op=ALU.pow),
                     reads=[f"mvt{sl}", "cst"], writes=[f"mvr{sl}"])

            def part2(j=j, sl=sl, mvs=mvs, banks=banks, src=src, names=names):
                S.op("dve", lambda e: e.scalar_tensor_tensor(out=mvs[:, 3:4], in0=mvs[:, 0:1], scalar=-1.0,
                                                             in1=mvs[:, 2:3], op0=ALU.mult, op1=ALU.mult),
                     reads=[f"mv{sl}", f"mvr{sl}"], writes=[f"mvt{sl}"])
                S.op("act", lambda e: e.activation(out=src[:], in_=src[:], func=AF.Identity, bias=mvs[:, 3:4],
                                                   scale=mvs[:, 2:3]),
                     reads=names + [f"mvt{sl}", f"mvr{sl}"], writes=names)
                S.op("dve", lambda e: e.tensor_tensor(out=src[:], in0=src[:], in1=tpar[:, 1, :], op=ALU.mult),
                     reads=names + ["tpar"], writes=names)
                S.op("pool", lambda e: e.tensor_tensor(out=vfull[:, j, :], in0=src[:], in1=tpar[:, 2, :], op=ALU.add),
                     reads=names + ["tpar"], writes=[f"vfull{j}"])

            chunk_step(part1, part2)

    def stage_p(ti):
        xT = xTs[ti % 2]
        XT_ALL = XTN[ti % 2]
        ensure_loaded(ti * NUNITS + 3)
        w0, n0 = wslot(ti, 0)
        w1, n1 = wslot(ti, 1)
        t, first, last = seg_pos(ti)
        wins = ([0] if first else []) + [1, 2, 3, 4]
        for n_w, win in enumerate(wins):
            m = 16 if (win == 4 and last) else 128
            sl = pslot(ti, win)
            for h, (w, wn) in enumerate(((w0, n0), (w1, n1))):
                b = next_bank()
                mm_group(b, lambda k, win=win, m=m: xT[:, k, 128 * win:128 * win + m], lambda k, w=w: w[:, k, :], 8,
                         XT_ALL + [wn], out_ap=ps[0:m, b, :])
                S.op("dve", lambda e, h=h, b=b, sl=sl, m=m: e.tensor_tensor(
                    out=pbuf[0:m, sl, h * 512:(h + 1) * 512], in0=tpar[0:m, 0, h * 512:(h + 1) * 512],
                    in1=ps[0:m, b, :], op=ALU.add), reads=[f"ps{b}", "tpar"], writes=[f"p{sl}h{h}"])
            if n_w == 0:
                run_deferred()

    def stage_zu(ti):
        xT = xTs[ti % 2]
        XT_ALL = XTN[ti % 2]
        for half in range(2):
            uz = 4 + 2 * half
            uu = 5 + 2 * half
            ensure_loaded(ti * NUNITS + uu + 2)
            wz, nz = wslot(ti, uz)
            wu, nu = wslot(ti, uu)
            for el in range(4):
                ec = half * 4 + el
                sl = ec % 2
                b = next_bank()
                mm_group(b, lambda k, el=el: wz[:, k, el * 128:(el + 1) * 128], lambda k: xT[:, k, 8:520], 8, XT_ALL + [nz])
                S.op("act", lambda e, b=b, sl=sl, ec=ec: e.activation(
                    out=sza[sl][:], in_=ps[:, b, :], func=AF.Silu, bias=ppar[:, CB_ZA + ec:CB_ZA + ec + 1], scale=1.0),
                    reads=[f"ps{b}", "ppar"], writes=[f"sza{sl}"])
                b2 = next_bank()
                mm_group(b2, lambda k, el=el: wu[:, k, el * 128:(el + 1) * 128], lambda k: xT[:, k, 8:520], 8, XT_ALL + [nu])
                S.op("dve", lambda e, b2=b2, sl=sl, ec=ec: e.scalar_tensor_tensor(
                    out=us[:, ec, :], in0=ps[:, b2, :], scalar=ppar[:, CB_U + ec:CB_U + ec + 1], in1=sza[sl][:],
                    op0=ALU.add, op1=ALU.mult), reads=[f"ps{b2}", "ppar", f"sza{sl}"], writes=[f"us{ec}"])
                if ec == 0:
                    run_deferred()

    def stage_spatial(ti, gs=range(8)):
        for g in gs:
            b = next_bank()
            o = ps[:, b, :]
            S.op("pe", lambda e, g=g, o=o: e.matmul(
                o.rearrange("p (j q) -> p j q", j=4), lhsT=ones33[:],
                rhs=bsp_rows[:, g * 128:(g + 1) * 128].unsqueeze(1).broadcast_to([128, 4, 128]),
                start=True, stop=False), reads=["ones33", "bsp_rows"], writes=[f"ps{b}"], inc=False)
            for j in range(NCH):
                S.op("pe", lambda e, g=g, j=j, o=o: e.matmul(
                    o[:, 128 * j:128 * j + 128], lhsT=vfull[:, j, g * 128:(g + 1) * 128], rhs=wspT[:, g, :],
                    start=False, stop=(j == NCH - 1)), reads=[f"vfull{j}", "wspT"], writes=[f"ps{b}"],
                    inc=(j == NCH - 1))
            S.op("dve", lambda e, g=g, b=b: e.tensor_tensor(out=bra[:, g, :], in0=us[:, g, :], in1=ps[:, b, :], op=ALU.mult),
                 reads=[f"ps{b}", f"us{g}"], writes=[f"bra{g}"])

    def chunk_variant(seg, cidx):
        nch = SEGS[seg] // 128
        if cidx == 0:
            return 1 if seg == 0 else 3
        if cidx == nch - 1:
            return 2 if seg == 0 else 4
        return 0

    def stage_pool(ti, ccs=range(8)):
        seg, t = tiles[ti]
        for cc in ccs:
            g = cc // 2
            b = next_bank()
            o = ps[:, b, :]
            for j in range(NCH):
                var = chunk_variant(seg, 4 * t + j)
                ops = []
                i_hi = (var * 2 + 0) * 4 + g
                i_lo = (var * 2 + 1) * 4 + g
                oj = o[:, 128 * j:128 * j + 128]
                oj2 = o[:, 128 * j + 112:128 * j + 128]
                s1, s2 = pslot(ti, j), pslot(ti, j + 1)
                l1 = pbuf[:, s1, cc * 128:(cc + 1) * 128]
                l2 = pbuf[:, s2, cc * 128:(cc + 1) * 128]
                ops.append((oj, l1, pa1[:, i_hi, :]))
                if var != 0:
                    ops.append((oj, l1, pa1[:, i_lo, :]))
                ops.append((oj2, l2, pa2[:, i_hi, :]))
                if var != 0:
                    ops.append((oj2, l2, pa2[:, i_lo, :]))
                rd = [f"p{s1}h{cc // 4}", f"p{s2}h{cc // 4}", "pa1", "pa2"]
                for n, (oo, ll, rr) in enumerate(ops):
                    last = (n == len(ops) - 1)
                    S.op("pe", lambda e, oo=oo, ll=ll, rr=rr, n=n, last=last: e.matmul(
                        oo, lhsT=ll, rhs=rr, start=(n == 0), stop=last),
                        reads=rd, writes=[f"ps{b}"], inc=(last and j == NCH - 1))
            S.op("act", lambda e, b=b, cc=cc: e.copy(out=pooled[:, cc, :], in_=ps[:, b, :]),
                 reads=[f"ps{b}"], writes=[f"pooled{cc}"])

    def stage_zb_m(ti, ecs=range(8)):
        xT = xTs[ti % 2]
        XT_ALL = XTN[ti % 2]
        for ec in ecs:
            half, el = ec // 4, ec % 4
            uzb = 8 + half
            ensure_loaded(ti * NUNITS + uzb + 2)
            wz, nz = wslot(ti, uzb)
            if True:
                g, dd = ec // 2, ec % 2
                sl = ec % 2
                b = next_bank()
                mm_group(b, lambda k, el=el: wz[:, k, el * 128:(el + 1) * 128], lambda k: xT[:, k, 8:520], 8, XT_ALL + [nz])
                S.op("act", lambda e, b=b, sl=sl, ec=ec: e.activation(
                    out=szb[sl][:], in_=ps[:, b, :], func=AF.Silu, bias=ppar[:, CB_ZB + ec:CB_ZB + ec + 1], scale=1.0),
                    reads=[f"ps{b}", "ppar"], writes=[f"szb{sl}"])
                b2 = next_bank()
                mm_group(b2, lambda k, g=g, dd=dd: wpool[:, g * 2 + k, dd * 128:(dd + 1) * 128],
                         lambda k, g=g: pooled[:, 2 * g + k, :], 2, ["wpool", f"pooled{2 * g}", f"pooled{2 * g + 1}"])
                S.op("dve", lambda e, b2=b2, sl=sl, ec=ec: e.scalar_tensor_tensor(
                    out=szb[sl][:], in0=ps[:, b2, :], scalar=ppar[:, 56 + ec:57 + ec], in1=szb[sl][:],
                    op0=ALU.add, op1=ALU.mult), reads=[f"ps{b2}", "ppar", f"szb{sl}"], writes=[f"szb{sl}"])
                S.op("act", lambda e, sl=sl, ec=ec: e.activation(
                    out=brb[:, ec, :], in_=szb[sl][:], func=AF.Identity, bias=0.0, scale=ppar[:, 64 + ec:65 + ec]),
                    reads=[f"szb{sl}", "ppar"], writes=[f"brb{ec}"])

    BRA_ALL = [f"bra{g}" for g in range(8)]
    BRB_ALL = [f"brb{g}" for g in range(8)]

    def stage_merge(ti):
        xT = xTs[ti % 2]
        XT_ALL = XTN[ti % 2]
        for q in range(4):
            ug = 10 + 2 * q
            ub = 11 + 2 * q
            ensure_loaded(ti * NUNITS + ub + 2)
            wg, ng = wslot(ti, ug)
            wb, nb = wslot(ti, ub)
            for dl in range(2):
                dd = 2 * q + dl
                sl = dd % 2
                b = next_bank()
                mm_group(b, lambda k, dl=dl: wg[:, k, dl * 128:(dl + 1) * 128], lambda k: xT[:, k, 8:520], 8, XT_ALL + [ng])
                S.op("act", lambda e, b=b, sl=sl, dd=dd: e.activation(
                    out=sga[sl][:], in_=ps[:, b, :], func=AF.Sigmoid, bias=ppar[:, CB_GA + dd:CB_GA + dd + 1], scale=1.0),
                    reads=[f"ps{b}", "ppar"], writes=[f"sga{sl}"])
                b = next_bank()
                mm_group(b, lambda k, dl=dl: wg[:, k, 256 + dl * 128:256 + (dl + 1) * 128], lambda k: xT[:, k, 8:520], 8,
                         XT_ALL + [ng])
                S.op("act", lambda e, b=b, sl=sl, dd=dd: e.activation(
                    out=sgb[sl][:], in_=ps[:, b, :], func=AF.Sigmoid, bias=ppar[:, CB_GB + dd:CB_GB + dd + 1], scale=1.0),
                    reads=[f"ps{b}", "ppar"], writes=[f"sgb{sl}"])
                b = next_bank()
                mm_group(b, lambda k, dl=dl: wb[:, k, dl * 128:(dl + 1) * 128], lambda k: bra[:, k, :], 8, BRA_ALL + [nb])
                S.op("dve", lambda e, b=b, sl=sl: e.tensor_tensor(out=sga[sl][:], in0=sga[sl][:], in1=ps[:, b, :], op=ALU.mult),
                     reads=[f"ps{b}", f"sga{sl}"], writes=[f"sga{sl}"])
                b = next_bank()
                mm_group(b, lambda k, dl=dl: wb[:, k, 256 + dl * 128:256 + (dl + 1) * 128], lambda k: brb[:, k, :], 8,
                         BRB_ALL + [nb])
                S.op("dve", lambda e, b=b, sl=sl: e.tensor_tensor(out=sgb[sl][:], in0=sgb[sl][:], in1=ps[:, b, :], op=ALU.mult),
                     reads=[f"ps{b}", f"sgb{sl}"], writes=[f"sgb{sl}"])
                S.op("pool", lambda e, sl=sl, dd=dd: e.tensor_tensor(out=merged[:, dd, :], in0=sga[sl][:], in1=sgb[sl][:], op=ALU.add),
                     reads=[f"sga{sl}", f"sgb{sl}"], writes=[f"merged{dd}"])

    MERGED_ALL = [f"merged{d}" for d in range(8)]

    def stage_xf_dma(ti):
        seg, t = tiles[ti]
        base = SEG_PAD_OFF[seg] + T * t + 8
        for j in range(NCH):
            S.dma("sp", xf[j][:], xpad[base + 128 * j:base + 128 * j + 128, :], f"xfld{j}", writes=[f"xf{j}h0", f"xf{j}h1"])

    def stage_xf_load(ti):
        for j in range(NCH):
            S.op("pool", lambda e, j=j: e.tensor_tensor(out=xf[j][:], in0=xf[j][:], in1=cst[:, 2:3].broadcast_to([128, D]), op=ALU.mult),
                 reads=[f"xf{j}h0", f"xf{j}h1", "cst"], writes=[f"xf{j}h0", f"xf{j}h1"])
            S.op("pool", lambda e, j=j: e.tensor_tensor(out=xf[j][:], in0=xf[j][:], in1=tpar[:, 3, :], op=ALU.add),
                 reads=[f"xf{j}h0", f"xf{j}h1", "tpar"], writes=[f"xf{j}h0", f"xf{j}h1"])

    def stage_out(ti):
        seg, t = tiles[ti]
        ensure_loaded(ti * NUNITS + 19 + 2)
        w0, n0 = wslot(ti, 18)
        w1, n1 = wslot(ti, 19)
        obase = SEG_OUT_OFF[seg] + T * t
        for j in range(NCH):
            sl = 4 + j
            mvs = mv[:, sl, :]
            names = [f"xf{j}h0", f"xf{j}h1"]

            def part1(j=j, sl=sl, mvs=mvs):
                hw = ((w0, n0), (w1, n1))
                bks = [next_bank(), next_bank()]
                for ks in (range(0, 7), range(7, 8)):
                    for h, (w, wn) in enumerate(hw):
                        for k in ks:
                            S.op("pe", lambda e, k=k, w=w, h=h: e.matmul(
                                ps[:, bks[h], :], lhsT=merged[:, k, 128 * j:128 * j + 128], rhs=w[:, k, :],
                                start=(k == 0), stop=(k == 7)),
                                reads=[f"merged{k}", wn], writes=[f"ps{bks[h]}"], inc=(k == 7))
                for h, (w, wn) in enumerate(hw):
                    b = bks[h]
                    S.op("dve", lambda e, h=h, b=b: e.tensor_tensor(
                        out=xf[j][:, h * 512:(h + 1) * 512], in0=xf[j][:, h * 512:(h + 1) * 512], in1=ps[:, b, :],
                        op=ALU.add), reads=[f"ps{b}", f"xf{j}h{h}"], writes=[f"xf{j}h{h}"])
                    S.op("dve", lambda e, h=h: e.bn_stats(out=stats[:, sl, h, :], in_=xf[j][:, h * 512:(h + 1) * 512]),
                         reads=[f"xf{j}h{h}"], writes=[f"stats{sl}"] if h == 1 else [f"stats{sl}a"])
                S.op("dve", lambda e: e.bn_aggr(out=mvs[:, 0:2], in_=stats[:, sl, :, :]),
                     reads=[f"stats{sl}", f"stats{sl}a"], writes=[f"mv{sl}"])
                S.op("pool", lambda e: e.tensor_tensor(out=mvs[:, 3:4], in0=mvs[:, 1:2], in1=cst[:, 0:1], op=ALU.add),
                     reads=[f"mv{sl}", "cst"], writes=[f"mvt{sl}"])
                S.op("pool", lambda e: e.tensor_tensor(out=mvs[:, 2:3], in0=mvs[:, 3:4], in1=cst[:, 1:2], op=ALU.pow),
                     reads=[f"mvt{sl}", "cst"], writes=[f"mvr{sl}"])

            def part2(j=j, sl=sl, mvs=mvs, names=names):
                S.op("dve", lambda e: e.scalar_tensor_tensor(out=mvs[:, 3:4], in0=mvs[:, 0:1], scalar=-1.0,
                                                             in1=mvs[:, 2:3], op0=ALU.mult, op1=ALU.mult),
                     reads=[f"mv{sl}", f"mvr{sl}"], writes=[f"mvt{sl}"])
                S.op("act", lambda e: e.activation(
                    out=xf[j][:], in_=xf[j][:], func=AF.Identity, bias=mvs[:, 3:4], scale=mvs[:, 2:3]),
                    reads=names + [f"mvt{sl}", f"mvr{sl}"], writes=names)
                S.op("dve", lambda e: e.tensor_tensor(out=xf[j][:], in0=xf[j][:], in1=tpar[:, 4, :], op=ALU.mult),
                     reads=names + ["tpar"], writes=names)
                S.op("pool", lambda e: e.tensor_tensor(out=xf[j][:], in0=xf[j][:], in1=tpar[:, 5, :], op=ALU.add),
                     reads=names + ["tpar"], writes=names)
                S.dma("act", y_d[obase + 128 * j:obase + 128 * j + 128, :], xf[j][:], f"yst{j}", reads=names)

            chunk_step(part1, part2)

    ntiles = len(tiles)
    S.prewait("pool", reads=["xbf0"])
    ensure_loaded(1)
    S.prewait("pool", reads=XTN[0])
    ensure_loaded(3)
    S.dma("pool", wspT[:], wspT_d, "c_wsp", writes=["wspT"])
    S.dma("pool", pa1[:], pa1_d, "c_pa1", writes=["pa1"])
    S.dma("pool", pa2[:], pa2_d, "c_pa2", writes=["pa2"])
    S.dma("pool", wpool[:], wpool_d, "c_wpool", writes=["wpool"])
    for ti in range(ntiles):
        if ti + 1 < ntiles:
            cast_x(ti + 1)
        stage_p(ti)
        stage_v(ti)
        stage_xf_dma(ti)
        stage_zu(ti)
        if ti + 1 < ntiles:
            transpose_x(ti + 1)
        for r in range(4):
            stage_spatial(ti, (2 * r, 2 * r + 1))
            stage_pool(ti, (2 * r, 2 * r + 1))
            stage_zb_m(ti, (2 * r, 2 * r + 1))
            if r == 0:
                stage_xf_load(ti)
        stage_merge(ti)
        stage_out(ti)
    run_deferred()
    S.wait_all_dma("act", [f"yst{j}" for j in range(NCH)])
    return nc


def _pool_mats(c0, L):
    a1 = np.zeros((4, 128, 128), np.float64)
    a2 = np.zeros((4, 16, 16), np.float64)
    for g, w in enumerate(POOL_WINDOWS):
        for t in range(128):
            tt = c0 + t
            lo = min(max(tt - w // 2, 0), L)
            hi = min(max(tt + w - w // 2, 0), L)
            cnt = hi - lo
            for st in range(lo, hi):
                _add(a1, a2, g, st, t, c0, 1.0 / cnt)
            _add(a1, a2, g, tt, t, c0, -1.0)
    return a1, a2


def _add(a1, a2, g, st, t, c0, val):
    s1 = st - (c0 - 8)
    if 0 <= s1 < 128:
        a1[g, s1, t] += val
    else:
        s2 = st - (c0 + 120)
        assert 0 <= s2 < 16 and t >= 112, (st, t, c0)
        a2[g, s2, t - 112] += val


def _hilo(a):
    a = a.astype(np.float32)
    hi = a.astype(ml_dtypes.bfloat16).astype(np.float32)
    lo = (a - hi).astype(ml_dtypes.bfloat16).astype(np.float32)
    return hi, lo


_NC_CACHE = {}


def kernel(x_prompt, x_sample, w_in, b_in, ln_v_g, ln_v_b, w_spatial, b_spatial, w_pool, b_pool,
           pool_scale, w_br_a, w_br_b, w_out, b_out, ln_g, ln_b):
    f32 = np.float32
    x_prompt = np.asarray(x_prompt, f32)
    x_sample = np.asarray(x_sample, f32)
    w_in = np.asarray(w_in, f32)[0]
    b_in = np.asarray(b_in, f32)[0]
    w_br_a = np.asarray(w_br_a, f32)[0]
    w_br_b = np.asarray(w_br_b, f32)[0]
    w_out = np.asarray(w_out, f32)[0]

    def unit_from(cols_list):
        m = np.concatenate(cols_list, axis=1)
        return m.reshape(8, 128, 512).transpose(1, 0, 2)

    units = []
    units += [unit_from([w_in[:, 3072 + 512 * h:3072 + 512 * (h + 1)]]) for h in range(2)]
    units += [unit_from([w_in[:, 1024 + 512 * h:1024 + 512 * (h + 1)]]) for h in range(2)]
    for h in range(2):
        units.append(unit_from([w_in[:, 2048 + 512 * h:2048 + 512 * (h + 1)]]))
        units.append(unit_from([w_in[:, 0 + 512 * h:512 * (h + 1)]]))
    units += [unit_from([w_in[:, 4096 + 512 * h:4096 + 512 * (h + 1)]]) for h in range(2)]
    for q in range(4):
        units.append(unit_from([w_in[:, 5120 + 256 * q:5120 + 256 * (q + 1)],
                                w_in[:, 6144 + 256 * q:6144 + 256 * (q + 1)]]))
        units.append(unit_from([w_br_a[:, 256 * q:256 * (q + 1)], w_br_b[:, 256 * q:256 * (q + 1)]]))
    units += [unit_from([w_out[:, 512 * h:512 * (h + 1)]]) for h in range(2)]
    wall = np.ascontiguousarray(np.stack(units, axis=0), dtype=f32)

    ppar = np.zeros((128, 72), f32)
    ppar[:, 0:56] = b_in.reshape(56, 128).T
    ppar[:, 56:64] = np.asarray(b_pool, f32).reshape(8, 128).T
    ppar[:, 64:72] = np.asarray(pool_scale, f32).reshape(8, 128).T
    tpar = np.stack([b_in[1024:2048], b_in[3072:4096], np.asarray(ln_v_g, f32)[0], np.asarray(ln_v_b, f32)[0],
                     np.asarray(b_out, f32)[0], np.asarray(ln_g, f32)[0], np.asarray(ln_b, f32)[0]], axis=0).astype(f32)
    bsp = np.asarray(b_spatial, f32).reshape(1, 1024)
    wspT = np.ascontiguousarray(np.asarray(w_spatial, f32)[0].transpose(2, 0, 1))
    wp = np.asarray(w_pool, f32)[0]
    wpool = np.ascontiguousarray(wp.reshape(4, 2, 128, 256).transpose(2, 0, 1, 3).reshape(128, 8, 256))

    in_maps = []
    for c in range(NCORES):
        b, qd = c // 4, c % 4
        xp = np.zeros((XPAD_ROWS, D), f32)
        lo, hi = qd * 4096 - 8, qd * 4096 + 4096 + 8
        slo, shi = max(lo, 0), min(hi, 16384)
        xp[slo - lo:slo - lo + (shi - slo)] = x_prompt[b, slo:shi]
        for k in range(2):
            off = SEG_PAD_OFF[1 + k] + 8
            xp[off:off + 2048] = x_sample[2 * c + k]
        specs = [(1024, 16384), (qd * 4096, 16384), (qd * 4096 + 4096 - 128, 16384), (0, 2048), (2048 - 128, 2048)]
        pa1 = np.zeros((128, 40, 128), f32)
        pa2 = np.zeros((128, 40, 16), f32)
        for v, (c0, L) in enumerate(specs):
            a1, a2 = _pool_mats(c0, L)
            h1, l1 = _hilo(a1)
            h2, l2 = _hilo(a2)
            for g in range(4):
                pa1[:, (v * 2 + 0) * 4 + g, :] = h1[g]
                pa1[:, (v * 2 + 1) * 4 + g, :] = l1[g]
                pa2[0:16, (v * 2 + 0) * 4 + g, :] = h2[g]
                pa2[0:16, (v * 2 + 1) * 4 + g, :] = l2[g]
        in_maps.append({"xpad": xp, "wall": wall, "ppar": ppar, "tpar": tpar, "bsp": bsp, "wspT": wspT,
                        "wpool": wpool, "pa1": pa1, "pa2": pa2})

    if "nc" not in _NC_CACHE:
        _NC_CACHE["nc"] = build_nc()
    nc = _NC_CACHE["nc"]
    res = run_bass_kernel_spmd(nc, in_maps, core_ids=list(range(NCORES)))
    y_prompt = np.zeros((2, 16384, D), f32)
    y_sample = np.zeros((16, 2048, D), f32)
    for c in range(NCORES):
        y = np.asarray(res.results[c]["y"], f32)
        b, qd = c // 4, c % 4
        y_prompt[b, qd * 4096:(qd + 1) * 4096] = y[0:4096]
        y_sample[2 * c] = y[4096:6144]
        y_sample[2 * c + 1] = y[6144:8192]
    return (y_prompt, y_sample)
```

```python
import numpy as np
import ml_dtypes
import concourse.bass as bass
import concourse.mybir as mybir
from concourse.bass_utils import run_bass_kernel_spmd

F32 = mybir.dt.float32
BF16 = mybir.dt.bfloat16
AF = mybir.ActivationFunctionType
ALU = mybir.AluOpType

D = 1024
NCORES = 8
T = 512
NCH = 4
SEGS = (4096, 2048, 2048)
SEG_PAD_OFF = (0, 4112, 6176)
SEG_OUT_OFF = (0, 4096, 6144)
XPAD_ROWS = 8240
NTOK = 8192
NUNITS = 20
NSLOT = 4
ALPHA = float(2.0 ** 0.25)
LN_EPS = 1e-5
POOL_WINDOWS = (2, 4, 8, 16)
CB_U, CB_V, CB_ZA, CB_P, CB_ZB, CB_GA, CB_GB = 0, 8, 16, 24, 32, 40, 48


class Sched:
    def __init__(self, nc):
        self.nc = nc
        self.eng = {"pe": nc.tensor, "act": nc.scalar, "dve": nc.vector,
                    "pool": nc.gpsimd, "sp": nc.sync}
        self.sem = {}
        self.tick = {}
        for e in ("pe", "act", "dve", "pool"):
            self.sem[e] = nc.alloc_semaphore("s_" + e)
            self.tick[e] = 0
        self.waited = {e: {} for e in self.eng}
        self.last_write = {}
        self.readers = {}
        self.dma_sem = {}
        self.dma_cnt = {}

    def _deps(self, reads, writes):
        deps = {}

        def add(k, v):
            if k not in deps or deps[k] < v:
                deps[k] = v

        for b in reads:
            lw = self.last_write.get(b)
            if lw is not None:
                add(*lw)
            if b.startswith("ps"):
                for k, v in self.readers.get(b, {}).items():
                    add(k, v)
        for b in writes:
            lw = self.last_write.get(b)
            if lw is not None:
                add(*lw)
            for k, v in self.readers.get(b, {}).items():
                add(k, v)
        return deps

    def _emit_waits(self, engine, deps, embed=False):
        need = []
        for k, v in deps.items():
            if k == "pe" and engine == "pe":
                continue
            if k in self.tick and v > self.tick[k]:
                raise RuntimeError(f"dep on unclosed milestone {k} {v} > {self.tick[k]}")
            if self.waited[engine].get(k, 0) >= v:
                continue
            sem = self.sem[k] if k in self.sem else self.dma_sem[k]
            need.append((sem, v))
            self.waited[engine][k] = v
        last = need.pop() if (embed and need) else None
        for sem, v in need:
            self.eng[engine].wait_ge(sem, v)
        return last

    def _record(self, key, val, reads, writes):
        for b in writes:
            self.last_write[b] = (key, val)
            self.readers[b] = {}
        for b in reads:
            r = self.readers.setdefault(b, {})
            if r.get(key, 0) < val:
                r[key] = val

    def op(self, engine, fn, reads=(), writes=(), inc=True):
        w = self._emit_waits(engine, self._deps(reads, writes), embed=True)
        ins = fn(self.eng[engine])
        if w is not None:
            ins._wait_ge(w[0], w[1])
        if inc:
            self.tick[engine] += 1
            ins.then_inc(self.sem[engine], 1)
            val = self.tick[engine]
        else:
            assert engine == "pe"
            val = self.tick[engine] + 1
        self._record(engine, val, reads, writes)
        return ins

    def prewait(self, engine, reads=(), writes=()):
        self._emit_waits(engine, self._deps(reads, writes))

    def dma(self, queue, out, in_, slot, reads=(), writes=(), transpose=False):
        w = self._emit_waits(queue, self._deps(reads, writes), embed=True)
        if slot not in self.dma_sem:
            self.dma_sem[slot] = self.nc.alloc_semaphore("d_" + slot)
            self.dma_cnt[slot] = 0
        if transpose:
            ins = self.eng[queue].dma_start_transpose(out=out, in_=in_)
        else:
            ins = self.eng[queue].dma_start(out=out, in_=in_)
        if w is not None:
            ins._wait_ge(w[0], w[1])
        self.dma_cnt[slot] += 16
        ins.then_inc(self.dma_sem[slot], 16)
        self._record(slot, self.dma_cnt[slot], reads, writes)
        return ins

    def wait_all_dma(self, engine, slots):
        for s in slots:
            if self.dma_cnt.get(s, 0) > 0 and self.waited[engine].get(s, 0) < self.dma_cnt[s]:
                self.eng[engine].wait_ge(self.dma_sem[s], self.dma_cnt[s])
                self.waited[engine][s] = self.dma_cnt[s]


def build_nc():
    nc = bass.Bass("TRN2", target_bir_lowering=False)
    S = Sched(nc)

    xpad = nc.dram_tensor("xpad", [XPAD_ROWS, D], F32, kind="ExternalInput").ap()
    wall = nc.dram_tensor("wall", [NUNITS, 128, 8, 512], F32, kind="ExternalInput").ap()
    ppar_d = nc.dram_tensor("ppar", [128, 72], F32, kind="ExternalInput").ap()
    tpar_d = nc.dram_tensor("tpar", [7, D], F32, kind="ExternalInput").ap()
    bsp_d = nc.dram_tensor("bsp", [1, D], F32, kind="ExternalInput").ap()
    wspT_d = nc.dram_tensor("wspT", [128, 8, 128], F32, kind="ExternalInput").ap()
    wpool_d = nc.dram_tensor("wpool", [128, 8, 256], F32, kind="ExternalInput").ap()
    pa1_d = nc.dram_tensor("pa1", [128, 40, 128], F32, kind="ExternalInput").ap()
    pa2_d = nc.dram_tensor("pa2", [128, 40, 16], F32, kind="ExternalInput").ap()
    y_d = nc.dram_tensor("y", [NTOK, D], F32, kind="ExternalOutput").ap()
    wall_bf = nc.dram_tensor("wall_bf", [NUNITS, 128, 8, 512], BF16, kind="Internal").ap()
    xbf = nc.dram_tensor("xbf", [2, 640, D], BF16, kind="Internal").ap()

    A = nc.alloc_sbuf_tensor
    pa1 = A("pa1_sb", [128, 40, 128], BF16)
    pa2 = A("pa2_sb", [128, 40, 16], BF16)
    ppar = A("ppar_sb", [128, 72], F32)
    tpar = A("tpar_sb", [128, 6, D], F32)
    wspT = A("wspT_sb", [128, 8, 128], BF16)
    wpool = A("wpool_sb", [128, 8, 256], BF16)
    bsp_rows = A("bsp_rows", [128, D], BF16)
    bv_rows = A("bv_rows", [128, D], BF16)
    ones33 = A("ones33", [128, 128], BF16)
    cst = A("cst", [128, 3], F32)
    ws = [A(f"ws{i}", [128, 8, 512], BF16) for i in range(NSLOT)]
    xTs = [A(f"xT{i}", [128, 8, 640], BF16) for i in range(2)]
    xf = [A(f"xf{i}", [128, D], F32) for i in range(4)]
    v32 = [A(f"v32_{i}", [128, D], F32) for i in range(2)]
    vfull = A("vfull", [128, NCH, D], BF16)
    pbuf = A("pbuf", [128, 5, D], BF16)
    sza = [A(f"sza{i}", [128, T], F32) for i in range(2)]
    szb = [A(f"szb{i}", [128, T], F32) for i in range(2)]
    us = A("us", [128, 8, T], F32)
    bra = A("bra", [128, 8, T], BF16)
    pooled = A("pooled", [128, 8, T], BF16)
    brb = A("brb", [128, 8, T], BF16)
    merged = A("merged", [128, 8, T], BF16)
    sga = [A(f"sga{i}", [128, T], F32) for i in range(2)]
    sgb = [A(f"sgb{i}", [128, T], F32) for i in range(2)]
    stats = A("stats", [128, 8, 2, 6], F32)
    mv = A("mv", [128, 8, 4], F32)
    ps = nc.alloc_psum_tensor("ps", [128, 8, 512], F32)
    bank_ctr = [0]

    def next_bank():
        b = bank_ctr[0] % 8
        bank_ctr[0] += 1
        return b

    XTN = [[f"xT{par}_{kc}" for kc in range(8)] for par in range(2)]

    def seg_pos(ti):
        seg, t = tiles[ti]
        nt = SEGS[seg] // T
        return t, (t == 0), (t == nt - 1)

    def x_rows(ti):
        return 528 if seg_pos(ti)[2] else 640

    def pslot(ti, win):
        return (4 * seg_pos(ti)[0] + win) % 5

    def cast_x(ti):
        seg, t = tiles[ti]
        base = SEG_PAD_OFF[seg] + T * t
        par = ti % 2
        R = x_rows(ti)
        S.dma("pool", xbf[par][0:R, :], xpad[base:base + R, :], f"xcast{par}", writes=[f"xbf{par}"])

    def transpose_x(ti):
        par = ti % 2
        R = x_rows(ti)
        for kc in range(8):
            S.dma("sp", xTs[par][:, kc, 0:R], xbf[par][0:R, kc * 128:(kc + 1) * 128], f"xtr{par}",
                  reads=[f"xbf{par}"], writes=[XTN[par][kc]], transpose=True)
        for kc in range(8):
            S.last_write[XTN[par][kc]] = (f"xtr{par}", S.dma_cnt[f"xtr{par}"])

    tiles = []
    for s, L in enumerate(SEGS):
        for t in range(L // T):
            tiles.append((s, t))
    cast_x(0)
    S.dma("sp", ppar[:], ppar_d, "c_ppar", writes=["ppar"])
    S.dma("sp", tpar[:].rearrange("p k d -> p (k d)"),
          tpar_d[1:7, :].rearrange("k d -> (k d)").partition_broadcast(128), "c_tpar", writes=["tpar"])
    transpose_x(0)

    def make_hilo_rows(src_row_ap, rows_tile, rows_name, sem, b32, n32, bhi, nhi, bhi32, nhi32):
        S.op("dve", lambda e: e.memset(b32, 0.0), writes=n32)
        S.dma("act", b32[0:1, :], src_row_ap, sem, reads=n32, writes=n32)
        S.dma("act", b32[32:33, :], src_row_ap, sem, writes=n32)
        S.op("dve", lambda e: e.memset(rows_tile[:], 0.0), writes=[rows_name])
        S.op("dve", lambda e: e.tensor_copy(out=bhi, in_=b32), reads=n32, writes=nhi)
        S.op("dve", lambda e: e.tensor_copy(out=bhi32, in_=bhi), reads=nhi, writes=nhi32)
        S.op("dve", lambda e: e.tensor_sub(out=b32, in0=b32, in1=bhi32), reads=n32 + nhi32, writes=n32)
        S.op("dve", lambda e: e.tensor_copy(out=rows_tile[0:1, :], in_=bhi[0:1, :]),
             reads=nhi + [rows_name], writes=[rows_name])
        S.op("dve", lambda e: e.tensor_copy(out=rows_tile[32:33, :], in_=b32[32:33, :]),
             reads=n32 + [rows_name], writes=[rows_name])

    make_hilo_rows(tpar_d[0:1, :], bv_rows, "bv_rows", "c_bv",
                   v32[0][0:33, :], ["v32_0h0", "v32_0h1"], pbuf[0:33, 0, :], ["p0h0", "p0h1"],
                   xf[0][0:33, :], ["xf0h0", "xf0h1"])
    make_hilo_rows(bsp_d, bsp_rows, "bsp_rows", "c_bsp",
                   v32[1][0:33, :], ["v32_1h0", "v32_1h1"], pbuf[0:33, 1, :], ["p1h0", "p1h1"],
                   xf[1][0:33, :], ["xf1h0", "xf1h1"])
    S.op("dve", lambda e: e.memset(ones33[:], 1.0), writes=["ones33"])
    S.op("dve", lambda e: e.memset(pbuf[:], 0.0), writes=[f"p{i}h{h}" for i in range(5) for h in range(2)])
    S.op("pool", lambda e: e.memset(cst[:, 0:1], LN_EPS), writes=["cst"])
    S.op("pool", lambda e: e.memset(cst[:, 1:2], -0.5), writes=["cst"])
    S.op("pool", lambda e: e.memset(cst[:, 2:3], ALPHA), writes=["cst"])

    wstate = {"next": 0}

    def ensure_loaded(upto):
        while wstate["next"] <= upto and wstate["next"] < NUNITS * len(tiles):
            U = wstate["next"]
            u = U % NUNITS
            sl = U % NSLOT
            if U < NUNITS:
                S.dma("pool", ws[sl][:], wall[u], f"wldc{sl}", writes=[f"ws{sl}"])
                S.dma("sp", wall_bf[u], ws[sl][:], f"wwb{u}", reads=[f"ws{sl}"], writes=[f"wall{u}"])
            else:
                S.dma("sp", ws[sl][:], wall_bf[u], f"wld{sl}", reads=[f"wall{u}"], writes=[f"ws{sl}"])
            wstate["next"] += 1

    def wslot(ti, u):
        U = ti * NUNITS + u
        return ws[U % NSLOT], f"ws{U % NSLOT}"

    deferred = []

    def run_deferred():
        while deferred:
            deferred.pop(0)()

    def chunk_step(part1, part2):
        part1()
        run_deferred()
        deferred.append(part2)

    def mm_group(b, lhs_fn, rhs_fn, nk, reads, out_ap=None):
        o = ps[:, b, :] if out_ap is None else out_ap
        for k in range(nk):
            S.op("pe", lambda e, k=k: e.matmul(o, lhsT=lhs_fn(k), rhs=rhs_fn(k), start=(k == 0), stop=(k == nk - 1)),
                 reads=reads, writes=[f"ps{b}"], inc=(k == nk - 1))

    def stage_v(ti):
        xT = xTs[ti % 2]
        XT_ALL = XTN[ti % 2]
        ensure_loaded(ti * NUNITS + 5)
        w0, n0 = wslot(ti, 2)
        w1, n1 = wslot(ti, 3)
        for j in range(NCH):
            sl = j
            vs = j % 2
            mvs = mv[:, sl, :]
            banks = []
            src = v32[vs]
            names = [f"v32_{vs}h0", f"v32_{vs}h1"]

            def part1(j=j, sl=sl, mvs=mvs, banks=banks, src=src, names=names):
                for h, (w, wn) in enumerate(((w0, n0), (w1, n1))):
                    b = next_bank()
                    banks.append(b)
                    o = ps[:, b, :]
                    S.op("pe", lambda e, h=h, o=o: e.matmul(o, lhsT=ones33[:], rhs=bv_rows[:, h * 512:(h + 1) * 512],
                                                            start=True, stop=False),
                         reads=["ones33", "bv_rows"], writes=[f"ps{b}"], inc=False)
                    for k in range(8):
                        S.op("pe", lambda e, k=k, w=w, o=o: e.matmul(
                            o, lhsT=xT[:, k, 8 + 128 * j:8 + 128 * j + 128], rhs=w[:, k, :], start=False, stop=(k == 7)),
                            reads=XT_ALL + [wn], writes=[f"ps{b}"], inc=(k == 7))
                    S.op("dve", lambda e, h=h, b=b: e.bn_stats(out=stats[:, sl, h, :], in_=ps[:, b, :]),
                         reads=[f"ps{b}"], writes=[f"stats{sl}"] if h == 1 else [f"stats{sl}a"])
                    S.op("act", lambda e, h=h, b=b: e.copy(out=src[:, h * 512:(h + 1) * 512], in_=ps[:, b, :]),
                         reads=[f"ps{b}"], writes=[names[h]])
                S.op("dve", lambda e: e.bn_aggr(out=mvs[:, 0:2], in_=stats[:, sl, :, :]),
                     reads=[f"stats{sl}", f"stats{sl}a"], writes=[f"mv{sl}"])
                S.op("pool", lambda e: e.tensor_tensor(out=mvs[:, 3:4], in0=mvs[:, 1:2], in1=cst[:, 0:1], op=ALU.add),
                     reads=[f"mv{sl}", "cst"], writes=[f"mvt{sl}"])
                S.op("pool", lambda e: e.tensor_tensor(out=mvs[:, 2:3], in0=mvs[:, 3:4], in1=cst[:, 1:2], op=ALU.pow),
                     reads=[f"mvt{sl}", "cst"], writes=[f"mvr{sl}"])

            def part2(j=j, sl=sl, mvs=mvs, banks=banks, src=src, names=names):
                S.op("dve", lambda e: e.scalar_tensor_tensor(out=mvs[:, 3:4], in0=mvs[:, 0:1], scalar=-1.0,
                                                             in1=mvs[:, 2:3], op0=ALU.mult, op1=ALU.mult),
                     reads=[f"mv{sl}", f"mvr{sl}"], writes=[f"mvt{sl}"])
                S.op("act", lambda e: e.activation(out=src[:], in_=src[:], func=AF.Identity, bias=mvs[:, 3:4],
                                                   scale=mvs[:, 2:3]),
                     reads=names + [f"mvt{sl}", f"mvr{sl}"], writes=names)
                S.op("dve", lambda e: e.tensor_tensor(out=src[:], in0=src[:], in1=tpar[:, 1, :], op=ALU.mult),
                     reads=names + ["tpar"], writes=names)
                S.op("pool", lambda e: e.tensor_tensor(out=vfull[:, j, :], in0=src[:], in1=tpar[:, 2, :], op=ALU.add),
                     reads=names + ["tpar"], writes=[f"vfull{j}"])

            chunk_step(part1, part2)

    def stage_p(ti):
        xT = xTs[ti % 2]
        XT_ALL = XTN[ti % 2]
        ensure_loaded(ti * NUNITS + 3)
        w0, n0 = wslot(ti, 0)
        w1, n1 = wslot(ti, 1)
        t, first, last = seg_pos(ti)
        wins = ([0] if first else []) + [1, 2, 3, 4]
        for n_w, win in enumerate(wins):
            m = 16 if (win == 4 and last) else 128
            sl = pslot(ti, win)
            for h, (w, wn) in enumerate(((w0, n0), (w1, n1))):
                b = next_bank()
                mm_group(b, lambda k, win=win, m=m: xT[:, k, 128 * win:128 * win + m], lambda k, w=w: w[:, k, :], 8,
                         XT_ALL + [wn], out_ap=ps[0:m, b, :])
                S.op("dve", lambda e, h=h, b=b, sl=sl, m=m: e.tensor_tensor(
                    out=pbuf[0:m, sl, h * 512:(h + 1) * 512], in0=tpar[0:m, 0, h * 512:(h + 1) * 512],
                    in1=ps[0:m, b, :], op=ALU.add), reads=[f"ps{b}", "tpar"], writes=[f"p{sl}h{h}"])
            if n_w == 0:
                run_deferred()

    def stage_zu(ti):
        xT = xTs[ti % 2]
        XT_ALL = XTN[ti % 2]
        for half in range(2):
            uz = 4 + 2 * half
            uu = 5 + 2 * half
            ensure_loaded(ti * NUNITS + uu + 2)
            wz, nz = wslot(ti, uz)
            wu, nu = wslot(ti, uu)
            for el in range(4):
                ec = half * 4 + el
                sl = ec % 2
                b = next_bank()
                mm_group(b, lambda k, el=el: wz[:, k, el * 128:(el + 1) * 128], lambda k: xT[:, k, 8:520], 8, XT_ALL + [nz])
                S.op("act", lambda e, b=b, sl=sl, ec=ec: e.activation(
                    out=sza[sl][:], in_=ps[:, b, :], func=AF.Silu, bias=ppar[:, CB_ZA + ec:CB_ZA + ec + 1], scale=1.0),
                    reads=[f"ps{b}", "ppar"], writes=[f"sza{sl}"])
                b2 = next_bank()
                mm_group(b2, lambda k, el=el: wu[:, k, el * 128:(el + 1) * 128], lambda k: xT[:, k, 8:520], 8, XT_ALL + [nu])
                S.op("dve", lambda e, b2=b2, sl=sl, ec=ec: e.scalar_tensor_tensor(
                    out=us[:, ec, :], in0=ps[:, b2, :], scalar=ppar[:, CB_U + ec:CB_U + ec + 1], in1=sza[sl][:],
                    op0=ALU.add, op1=ALU.mult), reads=[f"ps{b2}", "ppar", f"sza{sl}"], writes=[f"us{ec}"])
                if ec == 0:
                    run_deferred()

    def stage_spatial(ti, gs=range(8)):
        for g in gs:
            b = next_bank()
            o = ps[:, b, :]
            S.op("pe", lambda e, g=g, o=o: e.matmul(
                o.rearrange("p (j q) -> p j q", j=4), lhsT=ones33[:],
                rhs=bsp_rows[:, g * 128:(g + 1) * 128].unsqueeze(1).broadcast_to([128, 4, 128]),
                start=True, stop=False), reads=["ones33", "bsp_rows"], writes=[f"ps{b}"], inc=False)
            for j in range(NCH):
                S.op("pe", lambda e, g=g, j=j, o=o: e.matmul(
                    o[:, 128 * j:128 * j + 128], lhsT=vfull[:, j, g * 128:(g + 1) * 128], rhs=wspT[:, g, :],
                    start=False, stop=(j == NCH - 1)), reads=[f"vfull{j}", "wspT"], writes=[f"ps{b}"],
                    inc=(j == NCH - 1))
            S.op("dve", lambda e, g=g, b=b: e.tensor_tensor(out=bra[:, g, :], in0=us[:, g, :], in1=ps[:, b, :], op=ALU.mult),
                 reads=[f"ps{b}", f"us{g}"], writes=[f"bra{g}"])

    def chunk_variant(seg, cidx):
        nch = SEGS[seg] // 128
        if cidx == 0:
            return 1 if seg == 0 else 3
        if cidx == nch - 1:
            return 2 if seg == 0 else 4
        return 0

    def stage_pool(ti, ccs=range(8)):
        seg, t = tiles[ti]
        for cc in ccs:
            g = cc // 2
            b = next_bank()
            o = ps[:, b, :]
            for j in range(NCH):
                var = chunk_variant(seg, 4 * t + j)
                ops = []
                i_hi = (var * 2 + 0) * 4 + g
                i_lo = (var * 2 + 1) * 4 + g
                oj = o[:, 128 * j:128 * j + 128]
                oj2 = o[:, 128 * j + 112:128 * j + 128]
                s1, s2 = pslot(ti, j), pslot(ti, j + 1)
                l1 = pbuf[:, s1, cc * 128:(cc + 1) * 128]
                l2 = pbuf[:, s2, cc * 128:(cc + 1) * 128]
                ops.append((oj, l1, pa1[:, i_hi, :]))
                if var != 0:
                    ops.append((oj, l1, pa1[:, i_lo, :]))
                ops.append((oj2, l2, pa2[:, i_hi, :]))
                if var != 0:
                    ops.append((oj2, l2, pa2[:, i_lo, :]))
                rd = [f"p{s1}h{cc // 4}", f"p{s2}h{cc // 4}", "pa1", "pa2"]
                for n, (oo, ll, rr) in enumerate(ops):
                    last = (n == len(ops) - 1)
                    S.op("pe", lambda e, oo=oo, ll=ll, rr=rr, n=n, last=last: e.matmul(
                        oo, lhsT=ll, rhs=rr, start=(n == 0), stop=last),
                        reads=rd, writes=[f"ps{b}"], inc=(last and j == NCH - 1))
            S.op("act", lambda e, b=b, cc=cc: e.copy(out=pooled[:, cc, :], in_=ps[:, b, :]),
                 reads=[f"ps{b}"], writes=[f"pooled{cc}"])

    def stage_zb_m(ti, ecs=range(8)):
        xT = xTs[ti % 2]
        XT_ALL = XTN[ti % 2]
        for ec in ecs:
            half, el = ec // 4, ec % 4
            uzb = 8 + half
            ensure_loaded(ti * NUNITS + uzb + 2)
            wz, nz = wslot(ti, uzb)
            if True:
                g, dd = ec // 2, ec % 2
                sl = ec % 2
                b = next_bank()
                mm_group(b, lambda k, el=el: wz[:, k, el * 128:(el + 1) * 128], lambda k: xT[:, k, 8:520], 8, XT_ALL + [nz])
                S.op("act", lambda e, b=b, sl=sl, ec=ec: e.activation(
                    out=szb[sl][:], in_=ps[:, b, :], func=AF.Silu, bias=ppar[:, CB_ZB + ec:CB_ZB + ec + 1], scale=1.0),
                    reads=[f"ps{b}", "ppar"], writes=[f"szb{sl}"])
                b2 = next_bank()
                mm_group(b2, lambda k, g=g, dd=dd: wpool[:, g * 2 + k, dd * 128:(dd + 1) * 128],
                         lambda k, g=g: pooled[:, 2 * g + k, :], 2, ["wpool", f"pooled{2 * g}", f"pooled{2 * g + 1}"])
                S.op("dve", lambda e, b2=b2, sl=sl, ec=ec: e.scalar_tensor_tensor(
                    out=szb[sl][:], in0=ps[:, b2, :], scalar=ppar[:, 56 + ec:57 + ec], in1=szb[sl][:],
                    op0=ALU.add, op1=ALU.mult), reads=[f"ps{b2}", "ppar", f"szb{sl}"], writes=[f"szb{sl}"])
                S.op("act", lambda e, sl=sl, ec=ec: e.activation(
                    out=brb[:, ec, :], in_=szb[sl][:], func=AF.Identity, bias=0.0, scale=ppar[:, 64 + ec:65 + ec]),
                    reads=[f"szb{sl}", "ppar"], writes=[f"brb{ec}"])

    BRA_ALL = [f"bra{g}" for g in range(8)]
    BRB_ALL = [f"brb{g}" for g in range(8)]

    def stage_merge(ti):
        xT = xTs[ti % 2]
        XT_ALL = XTN[ti % 2]
        for q in range(4):
            ug = 10 + 2 * q
            ub = 11 + 2 * q
            ensure_loaded(ti * NUNITS + ub + 2)
            wg, ng = wslot(ti, ug)
            wb, nb = wslot(ti, ub)
            for dl in range(2):
                dd = 2 * q + dl
                sl = dd % 2
                b = next_bank()
                mm_group(b, lambda k, dl=dl: wg[:, k, dl * 128:(dl + 1) * 128], lambda k: xT[:, k, 8:520], 8, XT_ALL + [ng])
                S.op("act", lambda e, b=b, sl=sl, dd=dd: e.activation(
                    out=sga[sl][:], in_=ps[:, b, :], func=AF.Sigmoid, bias=ppar[:, CB_GA + dd:CB_GA + dd + 1], scale=1.0),
                    reads=[f"ps{b}", "ppar"], writes=[f"sga{sl}"])
                b = next_bank()
                mm_group(b, lambda k, dl=dl: wg[:, k, 256 + dl * 128:256 + (dl + 1) * 128], lambda k: xT[:, k, 8:520], 8,
                         XT_ALL + [ng])
                S.op("act", lambda e, b=b, sl=sl, dd=dd: e.activation(
                    out=sgb[sl][:], in_=ps[:, b, :], func=AF.Sigmoid, bias=ppar[:, CB_GB + dd:CB_GB + dd + 1], scale=1.0),
                    reads=[f"ps{b}", "ppar"], writes=[f"sgb{sl}"])
                b = next_bank()
                mm_group(b, lambda k, dl=dl: wb[:, k, dl * 128:(dl + 1) * 128], lambda k: bra[:, k, :], 8, BRA_ALL + [nb])
                S.op("dve", lambda e, b=b, sl=sl: e.tensor_tensor(out=sga[sl][:], in0=sga[sl][:], in1=ps[:, b, :], op=ALU.mult),
                     reads=[f"ps{b}", f"sga{sl}"], writes=[f"sga{sl}"])
                b = next_bank()
                mm_group(b, lambda k, dl=dl: wb[:, k, 256 + dl * 128:256 + (dl + 1) * 128], lambda k: brb[:, k, :], 8,
                         BRB_ALL + [nb])
                S.op("dve", lambda e, b=b, sl=sl: e.tensor_tensor(out=sgb[sl][:], in0=sgb[sl][:], in1=ps[:, b, :], op=ALU.mult),
                     reads=[f"ps{b}", f"sgb{sl}"], writes=[f"sgb{sl}"])
                S.op("pool", lambda e, sl=sl, dd=dd: e.tensor_tensor(out=merged[:, dd, :], in0=sga[sl][:], in1=sgb[sl][:], op=ALU.add),
                     reads=[f"sga{sl}", f"sgb{sl}"], writes=[f"merged{dd}"])

    MERGED_ALL = [f"merged{d}" for d in range(8)]

    def stage_xf_dma(ti):
        seg, t = tiles[ti]
        base = SEG_PAD_OFF[seg] + T * t + 8
        for j in range(NCH):
            S.dma("sp", xf[j][:], xpad[base + 128 * j:base + 128 * j + 128, :], f"xfld{j}", writes=[f"xf{j}h0", f"xf{j}h1"])

    def stage_xf_load(ti):
        for j in range(NCH):
            S.op("pool", lambda e, j=j: e.tensor_tensor(out=xf[j][:], in0=xf[j][:], in1=cst[:, 2:3].broadcast_to([128, D]), op=ALU.mult),
                 reads=[f"xf{j}h0", f"xf{j}h1", "cst"], writes=[f"xf{j}h0", f"xf{j}h1"])
            S.op("pool", lambda e, j=j: e.tensor_tensor(out=xf[j][:], in0=xf[j][:], in1=tpar[:, 3, :], op=ALU.add),
                 reads=[f"xf{j}h0", f"xf{j}h1", "tpar"], writes=[f"xf{j}h0", f"xf{j}h1"])

    def stage_out(ti):
        seg, t = tiles[ti]
        ensure_loaded(ti * NUNITS + 19 + 2)
        w0, n0 = wslot(ti, 18)
        w1, n1 = wslot(ti, 19)
        obase = SEG_OUT_OFF[seg] + T * t
        for j in range(NCH):
            sl = 4 + j
            mvs = mv[:, sl, :]
            names = [f"xf{j}h0", f"xf{j}h1"]

            def part1(j=j, sl=sl, mvs=mvs):
                hw = ((w0, n0), (w1, n1))
                bks = [next_bank(), next_bank()]
                for ks in (range(0, 7), range(7, 8)):
                    for h, (w, wn) in enumerate(hw):
                        for k in ks:
                            S.op("pe", lambda e, k=k, w=w, h=h: e.matmul(
                                ps[:, bks[h], :], lhsT=merged[:, k, 128 * j:128 * j + 128], rhs=w[:, k, :],
                                start=(k == 0), stop=(k == 7)),
                                reads=[f"merged{k}", wn], writes=[f"ps{bks[h]}"], inc=(k == 7))
                for h, (w, wn) in enumerate(hw):
                    b = bks[h]
                    S.op("dve", lambda e, h=h, b=b: e.tensor_tensor(
                        out=xf[j][:, h * 512:(h + 1) * 512], in0=xf[j][:, h * 512:(h + 1) * 512], in1=ps[:, b, :],
                        op=ALU.add), reads=[f"ps{b}", f"xf{j}h{h}"], writes=[f"xf{j}h{h}"])
                    S.op("dve", lambda e, h=h: e.bn_stats(out=stats[:, sl, h, :], in_=xf[j][:, h * 512:(h + 1) * 512]),
                         reads=[f"xf{j}h{h}"], writes=[f"stats{sl}"] if h == 1 else [f"stats{sl}a"])
                S.op("dve", lambda e: e.bn_aggr(out=mvs[:, 0:2], in_=stats[:, sl, :, :]),
                     reads=[f"stats{sl}", f"stats{sl}a"], writes=[f"mv{sl}"])
                S.op("pool", lambda e: e.tensor_tensor(out=mvs[:, 3:4], in0=mvs[:, 1:2], in1=cst[:, 0:1], op=ALU.add),
                     reads=[f"mv{sl}", "cst"], writes=[f"mvt{sl}"])
                S.op("pool", lambda e: e.tensor_tensor(out=mvs[:, 2:3], in0=mvs[:, 3:4], in1=cst[:, 1:2], op=ALU.pow),
                     reads=[f"mvt{sl}", "cst"], writes=[f"mvr{sl}"])

            def part2(j=j, sl=sl, mvs=mvs, names=names):
                S.op("dve", lambda e: e.scalar_tensor_tensor(out=mvs[:, 3:4], in0=mvs[:, 0:1], scalar=-1.0,
                                                             in1=mvs[:, 2:3], op0=ALU.mult, op1=ALU.mult),
                     reads=[f"mv{sl}", f"mvr{sl}"], writes=[f"mvt{sl}"])
                S.op("act", lambda e: e.activation(
                    out=xf[j][:], in_=xf[j][:], func=AF.Identity, bias=mvs[:, 3:4], scale=mvs[:, 2:3]),
                    reads=names + [f"mvt{sl}", f"mvr{sl}"], writes=names)
                S.op("dve", lambda e: e.tensor_tensor(out=xf[j][:], in0=xf[j][:], in1=tpar[:, 4, :], op=ALU.mult),
                     reads=names + ["tpar"], writes=names)
                S.op("pool", lambda e: e.tensor_tensor(out=xf[j][:], in0=xf[j][:], in1=tpar[:, 5, :], op=ALU.add),
                     reads=names + ["tpar"], writes=names)
                S.dma("act", y_d[obase + 128 * j:obase + 128 * j + 128, :], xf[j][:], f"yst{j}", reads=names)

            chunk_step(part1, part2)

    ntiles = len(tiles)
    S.prewait("pool", reads=XTN[0])
    ensure_loaded(1)
    S.prewait("pool", reads=XTN[0])
    ensure_loaded(3)
    S.dma("pool", wspT[:], wspT_d, "c_wsp", writes=["wspT"])
    S.dma("pool", pa1[:], pa1_d, "c_pa1", writes=["pa1"])
    S.dma("pool", pa2[:], pa2_d, "c_pa2", writes=["pa2"])
    S.dma("pool", wpool[:], wpool_d, "c_wpool", writes=["wpool"])
    for ti in range(ntiles):
        if ti + 1 < ntiles:
            cast_x(ti + 1)
        stage_p(ti)
        stage_v(ti)
        stage_xf_dma(ti)
        stage_zu(ti)
        if ti + 1 < ntiles:
            transpose_x(ti + 1)
        for r in range(4):
            stage_spatial(ti, (2 * r, 2 * r + 1))
            stage_pool(ti, (2 * r, 2 * r + 1))
            stage_zb_m(ti, (2 * r, 2 * r + 1))
            if r == 0:
                stage_xf_load(ti)
        stage_merge(ti)
        stage_out(ti)
    run_deferred()
    S.wait_all_dma("act", [f"yst{j}" for j in range(NCH)])
    return nc


def _pool_mats(c0, L):
    a1 = np.zeros((4, 128, 128), np.float64)
    a2 = np.zeros((4, 16, 16), np.float64)
    for g, w in enumerate(POOL_WINDOWS):
        for t in range(128):
            tt = c0 + t
            lo = min(max(tt - w // 2, 0), L)
            hi = min(max(tt + w - w // 2, 0), L)
            cnt = hi - lo
            for st in range(lo, hi):
                _add(a1, a2, g, st, t, c0, 1.0 / cnt)
            _add(a1, a2, g, tt, t, c0, -1.0)
    return a1, a2


def _add(a1, a2, g, st, t, c0, val):
    s1 = st - (c0 - 8)
    if 0 <= s1 < 128:
        a1[g, s1, t] += val
    else:
        s2 = st - (c0 + 120)
        assert 0 <= s2 < 16 and t >= 112, (st, t, c0)
        a2[g, s2, t - 112] += val


def _hilo(a):
    a = a.astype(np.float32)
    hi = a.astype(ml_dtypes.bfloat16).astype(np.float32)
    lo = (a - hi).astype(ml_dtypes.bfloat16).astype(np.float32)
    return hi, lo


_NC_CACHE = {}


def kernel(x_prompt, x_sample, w_in, b_in, ln_v_g, ln_v_b, w_spatial, b_spatial, w_pool, b_pool,
           pool_scale, w_br_a, w_br_b, w_out, b_out, ln_g, ln_b):
    f32 = np.float32
    x_prompt = np.asarray(x_prompt, f32)
    x_sample = np.asarray(x_sample, f32)
    w_in = np.asarray(w_in, f32)[0]
    b_in = np.asarray(b_in, f32)[0]
    w_br_a = np.asarray(w_br_a, f32)[0]
    w_br_b = np.asarray(w_br_b, f32)[0]
    w_out = np.asarray(w_out, f32)[0]

    def unit_from(cols_list):
        m = np.concatenate(cols_list, axis=1)
        return m.reshape(8, 128, 512).transpose(1, 0, 2)

    units = []
    units += [unit_from([w_in[:, 3072 + 512 * h:3072 + 512 * (h + 1)]]) for h in range(2)]
    units += [unit_from([w_in[:, 1024 + 512 * h:1024 + 512 * (h + 1)]]) for h in range(2)]
    for h in range(2):
        units.append(unit_from([w_in[:, 2048 + 512 * h:2048 + 512 * (h + 1)]]))
        units.append(unit_from([w_in[:, 0 + 512 * h:512 * (h + 1)]]))
    units += [unit_from([w_in[:, 4096 + 512 * h:4096 + 512 * (h + 1)]]) for h in range(2)]
    for q in range(4):
        units.append(unit_from([w_in[:, 5120 + 256 * q:5120 + 256 * (q + 1)],
                                w_in[:, 6144 + 256 * q:6144 + 256 * (q + 1)]]))
        units.append(unit_from([w_br_a[:, 256 * q:256 * (q + 1)], w_br_b[:, 256 * q:256 * (q + 1)]]))
    units += [unit_from([w_out[:, 512 * h:512 * (h + 1)]]) for h in range(2)]
    wall = np.ascontiguousarray(np.stack(units, axis=0), dtype=f32)

    ppar = np.zeros((128, 72), f32)
    ppar[:, 0:56] = b_in.reshape(56, 128).T
    ppar[:, 56:64] = np.asarray(b_pool, f32).reshape(8, 128).T
    ppar[:, 64:72] = np.asarray(pool_scale, f32).reshape(8, 128).T
    tpar = np.stack([b_in[1024:2048], b_in[3072:4096], np.asarray(ln_v_g, f32)[0], np.asarray(ln_v_b, f32)[0],
                     np.asarray(b_out, f32)[0], np.asarray(ln_g, f32)[0], np.asarray(ln_b, f32)[0]], axis=0).astype(f32)
    bsp = np.asarray(b_spatial, f32).reshape(1, 1024)
    wspT = np.ascontiguousarray(np.asarray(w_spatial, f32)[0].transpose(2, 0, 1))
    wp = np.asarray(w_pool, f32)[0]
    wpool = np.ascontiguousarray(wp.reshape(4, 2, 128, 256).transpose(2, 0, 1, 3).reshape(128, 8, 256))

    in_maps = []
    for c in range(NCORES):
        b, qd = c // 4, c % 4
        xp = np.zeros((XPAD_ROWS, D), f32)
        lo, hi = qd * 4096 - 8, qd * 4096 + 4096 + 8
        slo, shi = max(lo, 0), min(hi, 16384)
        xp[slo - lo:slo - lo + (shi - slo)] = x_prompt[b, slo:shi]
        for k in range(2):
            off = SEG_PAD_OFF[1 + k] + 8
            xp[off:off + 2048] = x_sample[2 * c + k]
        specs = [(1024, 16384), (qd * 4096, 16384), (qd * 4096 + 4096 - 128, 16384), (0, 2048), (2048 - 128, 2048)]
        pa1 = np.zeros((128, 40, 128), f32)
        pa2 = np.zeros((128, 40, 16), f32)
        for v, (c0, L) in enumerate(specs):
            a1, a2 = _pool_mats(c0, L)
            h1, l1 = _hilo(a1)
            h2, l2 = _hilo(a2)
            for g in range(4):
                pa1[:, (v * 2 + 0) * 4 + g, :] = h1[g]
                pa1[:, (v * 2 + 1) * 4 + g, :] = l1[g]
                pa2[0:16, (v * 2 + 0) * 4 + g, :] = h2[g]
                pa2[0:16, (v * 2 + 1) * 4 + g, :] = l2[g]
        in_maps.append({"xpad": xp, "wall": wall, "ppar": ppar, "tpar": tpar, "bsp": bsp, "wspT": wspT,
                        "wpool": wpool, "pa1": pa1, "pa2": pa2})

    if "nc" not in _NC_CACHE:
        _NC_CACHE["nc"] = build_nc()
    nc = _NC_CACHE["nc"]
    res = run_bass_kernel_spmd(nc, in_maps, core_ids=list(range(NCORES)))
    y_prompt = np.zeros((2, 16384, D), f32)
    y_sample = np.zeros((16, 2048, D), f32)
    for c in range(NCORES):
        y = np.asarray(res.results[c]["y"], f32)
        b, qd = c // 4, c % 4
        y_prompt[b, qd * 4096:(qd + 1) * 4096] = y[0:4096]
        y_sample[2 * c] = y[4096:6144]
        y_sample[2 * c + 1] = y[6144:8192]
    return (y_prompt, y_sample)
```

```python
import numpy as np
import ml_dtypes
import concourse.bass as bass
import concourse.mybir as mybir
from concourse.bass_utils import run_bass_kernel_spmd

F32 = mybir.dt.float32
BF16 = mybir.dt.bfloat16
AF = mybir.ActivationFunctionType
ALU = mybir.AluOpType

D = 1024
NCORES = 8
T = 512
NCH = 4
SEGS = (4096, 2048, 2048)
SEG_PAD_OFF = (0, 4112, 6176)
SEG_OUT_OFF = (0, 4096, 6144)
XPAD_ROWS = 8240
NTOK = 8192
NUNITS = 20
NSLOT = 4
ALPHA = float(2.0 ** 0.25)
LN_EPS = 1e-5
POOL_WINDOWS = (2, 4, 8, 16)
CB_U, CB_V, CB_ZA, CB_P, CB_ZB, CB_GA, CB_GB = 0, 8, 16, 24, 32, 40, 48


class Sched:
    def __init__(self, nc):
        self.nc = nc
        self.eng = {"pe": nc.tensor, "act": nc.scalar, "dve": nc.vector,
                    "pool": nc.gpsimd, "sp": nc.sync}
        self.sem = {}
        self.tick = {}
        for e in ("pe", "act", "dve", "pool"):
            self.sem[e] = nc.alloc_semaphore("s_" + e)
            self.tick[e] = 0
        self.waited = {e: {} for e in self.eng}
        self.last_write = {}
        self.readers = {}
        self.dma_sem = {}
        self.dma_cnt = {}

    def _deps(self, reads, writes):
        deps = {}

        def add(k, v):
            if k not in deps or deps[k] < v:
                deps[k] = v

        for b in reads:
            lw = self.last_write.get(b)
            if lw is not None:
                add(*lw)
            if b.startswith("ps"):
                for k, v in self.readers.get(b, {}).items():
                    add(k, v)
        for b in writes:
            lw = self.last_write.get(b)
            if lw is not None:
                add(*lw)
            for k, v in self.readers.get(b, {}).items():
                add(k, v)
        return deps

    def _emit_waits(self, engine, deps, embed=False):
        need = []
        for k, v in deps.items():
            if k == "pe" and engine == "pe":
                continue
            if k in self.tick and v > self.tick[k]:
                raise RuntimeError(f"dep on unclosed milestone {k} {v} > {self.tick[k]}")
            if self.waited[engine].get(k, 0) >= v:
                continue
            sem = self.sem[k] if k in self.sem else self.dma_sem[k]
            need.append((sem, v))
            self.waited[engine][k] = v
        last = need.pop() if (embed and need) else None
        for sem, v in need:
            self.eng[engine].wait_ge(sem, v)
        return last

    def _record(self, key, val, reads, writes):
        for b in writes:
            self.last_write[b] = (key, val)
            self.readers[b] = {}
        for b in reads:
            r = self.readers.setdefault(b, {})
            if r.get(key, 0) < val:
                r[key] = val

    def op(self, engine, fn, reads=(), writes=(), inc=True):
        w = self._emit_waits(engine, self._deps(reads, writes), embed=True)
        ins = fn(self.eng[engine])
        if w is not None:
            ins._wait_ge(w[0], w[1])
        if inc:
            self.tick[engine] += 1
            ins.then_inc(self.sem[engine], 1)
            val = self.tick[engine]
        else:
            assert engine == "pe"
            val = self.tick[engine] + 1
        self._record(engine, val, reads, writes)
        return ins

    def prewait(self, engine, reads=(), writes=()):
        self._emit_waits(engine, self._deps(reads, writes))

    def dma(self, queue, out, in_, slot, reads=(), writes=(), transpose=False):
        w = self._emit_waits(queue, self._deps(reads, writes), embed=True)
        if slot not in self.dma_sem:
            self.dma_sem[slot] = self.nc.alloc_semaphore("d_" + slot)
            self.dma_cnt[slot] = 0
        if transpose:
            ins = self.eng[queue].dma_start_transpose(out=out, in_=in_)
        else:
            ins = self.eng[queue].dma_start(out=out, in_=in_)
        if w is not None:
            ins._wait_ge(w[0], w[1])
        self.dma_cnt[slot] += 16
        ins.then_inc(self.dma_sem[slot], 16)
        self._record(slot, self.dma_cnt[slot], reads, writes)
        return ins

    def wait_all_dma(self, engine, slots):
        for s in slots:
            if self.dma_cnt.get(s, 0) > 0 and self.waited[engine].get(s, 0) < self.dma_cnt[s]:
                self.eng[engine].wait_ge(self.dma_sem[s], self.dma_cnt[s])
                self.waited[engine][s] = self.dma_cnt[s]


def build_nc():
    nc = bass.Bass("TRN2", target_bir_lowering=False)
    S = Sched(nc)

    xpad = nc.dram_tensor("xpad", [XPAD_ROWS, D], F32, kind="ExternalInput").ap()
    wall = nc.dram_tensor("wall", [NUNITS, 128, 8, 512], F32, kind="ExternalInput").ap()
    ppar_d = nc.dram_tensor("ppar", [128, 72], F32, kind="ExternalInput").ap()
    tpar_d = nc.dram_tensor("tpar", [7, D], F32, kind="ExternalInput").ap()
    bsp_d = nc.dram_tensor("bsp", [1, D], F32, kind="ExternalInput").ap()
    wspT_d = nc.dram_tensor("wspT", [128, 8, 128], F32, kind="ExternalInput").ap()
    wpool_d = nc.dram_tensor("wpool", [128, 8, 256], F32, kind="ExternalInput").ap()
    pa1_d = nc.dram_tensor("pa1", [128, 40, 128], F32, kind="ExternalInput").ap()
    pa2_d = nc.dram_tensor("pa2", [128, 40, 16], F32, kind="ExternalInput").ap()
    y_d = nc.dram_tensor("y", [NTOK, D], F32, kind="ExternalOutput").ap()
    wall_bf = nc.dram_tensor("wall_bf", [NUNITS, 128, 8, 512], BF16, kind="Internal").ap()
    xbf = nc.dram_tensor("xbf", [2, 640, D], BF16, kind="Internal").ap()

    A = nc.alloc_sbuf_tensor
    pa1 = A("pa1_sb", [128, 40, 128], BF16)
    pa2 = A("pa2_sb", [128, 40, 16], BF16)
    ppar = A("ppar_sb", [128, 72], F32)
    tpar = A("tpar_sb", [128, 6, D], F32)
    wspT = A("wspT_sb", [128, 8, 128], BF16)
    wpool = A("wpool_sb", [128, 8, 256], BF16)
    bsp_rows = A("bsp_rows", [128, D], BF16)
    bv_rows = A("bv_rows", [128, D], BF16)
    ones33 = A("ones33", [128, 128], BF16)
    cst = A("cst", [128, 3], F32)
    ws = [A(f"ws{i}", [128, 8, 512], BF16) for i in range(NSLOT)]
    xTs = [A(f"xT{i}", [128, 8, 640], BF16) for i in range(2)]
    xf = [A(f"xf{i}", [128, D], F32) for i in range(4)]
    v32 = [A(f"v32_{i}", [128, D], F32) for i in range(2)]
    vfull = A("vfull", [128, NCH, D], BF16)
    pbuf = A("pbuf", [128, 5, D], BF16)
    sza = [A(f"sza{i}", [128, T], F32) for i in range(2)]
    szb = [A(f"szb{i}", [128, T], F32) for i in range(2)]
    us = A("us", [128, 8, T], F32)
    bra = A("bra", [128, 8, T], BF16)
    pooled = A("pooled", [128, 8, T], BF16)
    brb = A("brb", [128, 8, T], BF16)
    merged = A("merged", [128, 8, T], BF16)
    sga = [A(f"sga{i}", [128, T], F32) for i in range(2)]
    sgb = [A(f"sgb{i}", [128, T], F32) for i in range(2)]
    stats = A("stats", [128, 8, 2, 6], F32)
    mv = A("mv", [128, 8, 4], F32)
    ps = nc.alloc_psum_tensor("ps", [128, 8, 512], F32)
    bank_ctr = [0]

    def next_bank():
        b = bank_ctr[0] % 8
        bank_ctr[0] += 1
        return b

    XTN = [[f"xT{par}_{kc}" for kc in range(8)] for par in range(2)]

    def seg_pos(ti):
        seg, t = tiles[ti]
        nt = SEGS[seg] // T
        return t, (t == 0), (t == nt - 1)

    def x_rows(ti):
        return 528 if seg_pos(ti)[2] else 640

    def pslot(ti, win):
        return (4 * seg_pos(ti)[0] + win) % 5

    def cast_x(ti):
        seg, t = tiles[ti]
        base = SEG_PAD_OFF[seg] + T * t
        par = ti % 2
        R = x_rows(ti)
        if ti == 0:
            for hh in range(2):
                S.dma("pool", xbf[par][0:R, 512 * hh:512 * (hh + 1)], xpad[base:base + R, 512 * hh:512 * (hh + 1)],
                      f"xcast0h{hh}", writes=[f"xbf0h{hh}"])
            return
        S.dma("pool", xbf[par][0:R, :], xpad[base:base + R, :], f"xcast{par}", writes=[f"xbf{par}", "xbf0h0", "xbf0h1"] if par == 0 else [f"xbf{par}"])

    def transpose_x(ti):
        par = ti % 2
        R = x_rows(ti)
        for kc in range(8):
            rd = [f"xbf0h{kc // 4}"] if ti == 0 else ([f"xbf{par}", "xbf0h0", "xbf0h1"] if par == 0 else [f"xbf{par}"])
            S.dma("sp", xTs[par][:, kc, 0:R], xbf[par][0:R, kc * 128:(kc + 1) * 128], f"xtr{par}",
                  reads=rd, writes=[XTN[par][kc]], transpose=True)
        for kc in range(8):
            S.last_write[XTN[par][kc]] = (f"xtr{par}", S.dma_cnt[f"xtr{par}"])

    tiles = []
    for s, L in enumerate(SEGS):
        for t in range(L // T):
            tiles.append((s, t))
    cast_x(0)
    S.dma("sp", ppar[:], ppar_d, "c_ppar", writes=["ppar"])
    S.dma("sp", tpar[:].rearrange("p k d -> p (k d)"),
          tpar_d[1:7, :].rearrange("k d -> (k d)").partition_broadcast(128), "c_tpar", writes=["tpar"])
    transpose_x(0)

    def make_hilo_rows(src_row_ap, rows_tile, rows_name, sem, b32, n32, bhi, nhi, bhi32, nhi32):
        S.op("dve", lambda e: e.memset(b32, 0.0), writes=n32)
        S.dma("act", b32[0:1, :], src_row_ap, sem, reads=n32, writes=n32)
        S.dma("act", b32[32:33, :], src_row_ap, sem, writes=n32)
        S.op("dve", lambda e: e.memset(rows_tile[:], 0.0), writes=[rows_name])
        S.op("dve", lambda e: e.tensor_copy(out=bhi, in_=b32), reads=n32, writes=nhi)
        S.op("dve", lambda e: e.tensor_copy(out=bhi32, in_=bhi), reads=nhi, writes=nhi32)
        S.op("dve", lambda e: e.tensor_sub(out=b32, in0=b32, in1=bhi32), reads=n32 + nhi32, writes=n32)
        S.op("dve", lambda e: e.tensor_copy(out=rows_tile[0:1, :], in_=bhi[0:1, :]),
             reads=nhi + [rows_name], writes=[rows_name])
        S.op("dve", lambda e: e.tensor_copy(out=rows_tile[32:33, :], in_=b32[32:33, :]),
             reads=n32 + [rows_name], writes=[rows_name])

    make_hilo_rows(tpar_d[0:1, :], bv_rows, "bv_rows", "c_bv",
                   v32[0][0:33, :], ["v32_0h0", "v32_0h1"], pbuf[0:33, 0, :], ["p0h0", "p0h1"],
                   xf[0][0:33, :], ["xf0h0", "xf0h1"])
    make_hilo_rows(bsp_d, bsp_rows, "bsp_rows", "c_bsp",
                   v32[1][0:33, :], ["v32_1h0", "v32_1h1"], pbuf[0:33, 1, :], ["p1h0", "p1h1"],
                   xf[1][0:33, :], ["xf1h0", "xf1h1"])
    S.op("dve", lambda e: e.memset(ones33[:], 1.0), writes=["ones33"])
    S.op("dve", lambda e: e.memset(pbuf[:], 0.0), writes=[f"p{i}h{h}" for i in range(5) for h in range(2)])
    S.op("pool", lambda e: e.memset(cst[:, 0:1], LN_EPS), writes=["cst"])
    S.op("pool", lambda e: e.memset(cst[:, 1:2], -0.5), writes=["cst"])
    S.op("pool", lambda e: e.memset(cst[:, 2:3], ALPHA), writes=["cst"])

    wstate = {"next": 0}

    def ensure_loaded(upto):
        while wstate["next"] <= upto and wstate["next"] < NUNITS * len(tiles):
            U = wstate["next"]
            u = U % NUNITS
            sl = U % NSLOT
            if U < NUNITS:
                S.dma("pool", ws[sl][:], wall[u], f"wldc{sl}", writes=[f"ws{sl}"])
                S.dma("sp", wall_bf[u], ws[sl][:], f"wwb{u}", reads=[f"ws{sl}"], writes=[f"wall{u}"])
            else:
                S.dma("sp", ws[sl][:], wall_bf[u], f"wld{sl}", reads=[f"wall{u}"], writes=[f"ws{sl}"])
            wstate["next"] += 1

    def wslot(ti, u):
        U = ti * NUNITS + u
        return ws[U % NSLOT], f"ws{U % NSLOT}"

    deferred = []

    def run_deferred():
        while deferred:
            deferred.pop(0)()

    def chunk_step(part1, part2):
        part1()
        run_deferred()
        deferred.append(part2)

    def mm_group(b, lhs_fn, rhs_fn, nk, reads, out_ap=None):
        o = ps[:, b, :] if out_ap is None else out_ap
        for k in range(nk):
            S.op("pe", lambda e, k=k: e.matmul(o, lhsT=lhs_fn(k), rhs=rhs_fn(k), start=(k == 0), stop=(k == nk - 1)),
                 reads=reads, writes=[f"ps{b}"], inc=(k == nk - 1))

    def stage_v(ti):
        xT = xTs[ti % 2]
        XT_ALL = XTN[ti % 2]
        ensure_loaded(ti * NUNITS + 5)
        w0, n0 = wslot(ti, 2)
        w1, n1 = wslot(ti, 3)
        for j in range(NCH):
            sl = j
            vs = j % 2
            mvs = mv[:, sl, :]
            banks = []
            src = v32[vs]
            names = [f"v32_{vs}h0", f"v32_{vs}h1"]

            def part1(j=j, sl=sl, mvs=mvs, banks=banks, src=src, names=names):
                for h, (w, wn) in enumerate(((w0, n0), (w1, n1))):
                    b = next_bank()
                    banks.append(b)
                    o = ps[:, b, :]
                    S.op("pe", lambda e, h=h, o=o: e.matmul(o, lhsT=ones33[:], rhs=bv_rows[:, h * 512:(h + 1) * 512],
                                                            start=True, stop=False),
                         reads=["ones33", "bv_rows"], writes=[f"ps{b}"], inc=False)
                    for k in range(8):
                        S.op("pe", lambda e, k=k, w=w, o=o: e.matmul(
                            o, lhsT=xT[:, k, 8 + 128 * j:8 + 128 * j + 128], rhs=w[:, k, :], start=False, stop=(k == 7)),
                            reads=XT_ALL + [wn], writes=[f"ps{b}"], inc=(k == 7))
                    S.op("dve", lambda e, h=h, b=b: e.bn_stats(out=stats[:, sl, h, :], in_=ps[:, b, :]),
                         reads=[f"ps{b}"], writes=[f"stats{sl}"] if h == 1 else [f"stats{sl}a"])
                    S.op("act", lambda e, h=h, b=b: e.copy(out=src[:, h * 512:(h + 1) * 512], in_=ps[:, b, :]),
                         reads=[f"ps{b}"], writes=[names[h]])
                S.op("dve", lambda e: e.bn_aggr(out=mvs[:, 0:2], in_=stats[:, sl, :, :]),
                     reads=[f"stats{sl}", f"stats{sl}a"], writes=[f"mv{sl}"])
                S.op("pool", lambda e: e.tensor_tensor(out=mvs[:, 3:4], in0=mvs[:, 1:2], in1=cst[:, 0:1], op=ALU.add),
                     reads=[f"mv{sl}", "cst"], writes=[f"mvt{sl}"])
                S.op("pool", lambda e: e.tensor_tensor(out=mvs[:, 2:3], in0=mvs[:, 3:4], in1=cst[:, 1:2], op=ALU.pow),
                     reads=[f"mvt{sl}", "cst"], writes=[f"mvr{sl}"])

            def part2(j=j, sl=sl, mvs=mvs, banks=banks, src=src, names=names):
                S.op("dve", lambda e: e.scalar_tensor_tensor(out=mvs[:, 3:4], in0=mvs[:, 0:1], scalar=-1.0,
                                                             in1=mvs[:, 2:3], op0=ALU.mult, op1=ALU.mult),
                     reads=[f"mv{sl}", f"mvr{sl}"], writes=[f"mvt{sl}"])
                S.op("act", lambda e: e.activation(out=src[:], in_=src[:], func=AF.Identity, bias=mvs[:, 3:4],
                                                   scale=mvs[:, 2:3]),
                     reads=names + [f"mvt{sl}", f"mvr{sl}"], writes=names)
                S.op("dve", lambda e: e.tensor_tensor(out=src[:], in0=src[:], in1=tpar[:, 1, :], op=ALU.mult),
                     reads=names + ["tpar"], writes=names)
                S.op("pool", lambda e: e.tensor_tensor(out=vfull[:, j, :], in0=src[:], in1=tpar[:, 2, :], op=ALU.add),
                     reads=names + ["tpar"], writes=[f"vfull{j}"])

            chunk_step(part1, part2)

    def stage_p(ti):
        xT = xTs[ti % 2]
        XT_ALL = XTN[ti % 2]
        ensure_loaded(ti * NUNITS + 3)
        w0, n0 = wslot(ti, 0)
        w1, n1 = wslot(ti, 1)
        t, first, last = seg_pos(ti)
        wins = ([0] if first else []) + [1, 2, 3, 4]
        for n_w, win in enumerate(wins):
            m = 16 if (win == 4 and last) else 128
            sl = pslot(ti, win)
            for h, (w, wn) in enumerate(((w0, n0), (w1, n1))):
                b = next_bank()
                mm_group(b, lambda k, win=win, m=m: xT[:, k, 128 * win:128 * win + m], lambda k, w=w: w[:, k, :], 8,
                         XT_ALL + [wn], out_ap=ps[0:m, b, :])
                S.op("dve", lambda e, h=h, b=b, sl=sl, m=m: e.tensor_tensor(
                    out=pbuf[0:m, sl, h * 512:(h + 1) * 512], in0=tpar[0:m, 0, h * 512:(h + 1) * 512],
                    in1=ps[0:m, b, :], op=ALU.add), reads=[f"ps{b}", "tpar"], writes=[f"p{sl}h{h}"])
            if n_w == 0:
                run_deferred()

    def stage_zu(ti):
        xT = xTs[ti % 2]
        XT_ALL = XTN[ti % 2]
        for half in range(2):
            uz = 4 + 2 * half
            uu = 5 + 2 * half
            ensure_loaded(ti * NUNITS + uu + 2)
            wz, nz = wslot(ti, uz)
            wu, nu = wslot(ti, uu)
            for el in range(4):
                ec = half * 4 + el
                sl = ec % 2
                b = next_bank()
                mm_group(b, lambda k, el=el: wz[:, k, el * 128:(el + 1) * 128], lambda k: xT[:, k, 8:520], 8, XT_ALL + [nz])
                S.op("act", lambda e, b=b, sl=sl, ec=ec: e.activation(
                    out=sza[sl][:], in_=ps[:, b, :], func=AF.Silu, bias=ppar[:, CB_ZA + ec:CB_ZA + ec + 1], scale=1.0),
                    reads=[f"ps{b}", "ppar"], writes=[f"sza{sl}"])
                b2 = next_bank()
                mm_group(b2, lambda k, el=el: wu[:, k, el * 128:(el + 1) * 128], lambda k: xT[:, k, 8:520], 8, XT_ALL + [nu])
                S.op("dve", lambda e, b2=b2, sl=sl, ec=ec: e.scalar_tensor_tensor(
                    out=us[:, ec, :], in0=ps[:, b2, :], scalar=ppar[:, CB_U + ec:CB_U + ec + 1], in1=sza[sl][:],
                    op0=ALU.add, op1=ALU.mult), reads=[f"ps{b2}", "ppar", f"sza{sl}"], writes=[f"us{ec}"])
                if ec == 0:
                    run_deferred()

    def stage_spatial(ti, gs=range(8)):
        for g in gs:
            b = next_bank()
            o = ps[:, b, :]
            S.op("pe", lambda e, g=g, o=o: e.matmul(
                o.rearrange("p (j q) -> p j q", j=4), lhsT=ones33[:],
                rhs=bsp_rows[:, g * 128:(g + 1) * 128].unsqueeze(1).broadcast_to([128, 4, 128]),
                start=True, stop=False), reads=["ones33", "bsp_rows"], writes=[f"ps{b}"], inc=False)
            for j in range(NCH):
                S.op("pe", lambda e, g=g, j=j, o=o: e.matmul(
                    o[:, 128 * j:128 * j + 128], lhsT=vfull[:, j, g * 128:(g + 1) * 128], rhs=wspT[:, g, :],
                    start=False, stop=(j == NCH - 1)), reads=[f"vfull{j}", "wspT"], writes=[f"ps{b}"],
                    inc=(j == NCH - 1))
            S.op("dve", lambda e, g=g, b=b: e.tensor_tensor(out=bra[:, g, :], in0=us[:, g, :], in1=ps[:, b, :], op=ALU.mult),
                 reads=[f"ps{b}", f"us{g}"], writes=[f"bra{g}"])

    def chunk_variant(seg, cidx):
        nch = SEGS[seg] // 128
        if cidx == 0:
            return 1 if seg == 0 else 3
        if cidx == nch - 1:
            return 2 if seg == 0 else 4
        return 0

    def stage_pool(ti, ccs=range(8)):
        seg, t = tiles[ti]
        for cc in ccs:
            g = cc // 2
            b = next_bank()
            o = ps[:, b, :]
            for j in range(NCH):
                var = chunk_variant(seg, 4 * t + j)
                ops = []
                i_hi = (var * 2 + 0) * 4 + g
                i_lo = (var * 2 + 1) * 4 + g
                oj = o[:, 128 * j:128 * j + 128]
                oj2 = o[:, 128 * j + 112:128 * j + 128]
                s1, s2 = pslot(ti, j), pslot(ti, j + 1)
                l1 = pbuf[:, s1, cc * 128:(cc + 1) * 128]
                l2 = pbuf[:, s2, cc * 128:(cc + 1) * 128]
                ops.append((oj, l1, pa1[:, i_hi, :]))
                if var != 0:
                    ops.append((oj, l1, pa1[:, i_lo, :]))
                ops.append((oj2, l2, pa2[:, i_hi, :]))
                if var != 0:
                    ops.append((oj2, l2, pa2[:, i_lo, :]))
                rd = [f"p{s1}h{cc // 4}", f"p{s2}h{cc // 4}", "pa1", "pa2"]
                for n, (oo, ll, rr) in enumerate(ops):
                    last = (n == len(ops) - 1)
                    S.op("pe", lambda e, oo=oo, ll=ll, rr=rr, n=n, last=last: e.matmul(
                        oo, lhsT=ll, rhs=rr, start=(n == 0), stop=last),
                        reads=rd, writes=[f"ps{b}"], inc=(last and j == NCH - 1))
            S.op("act", lambda e, b=b, cc=cc: e.copy(out=pooled[:, cc, :], in_=ps[:, b, :]),
                 reads=[f"ps{b}"], writes=[f"pooled{cc}"])

    def stage_zb_m(ti, ecs=range(8)):
        xT = xTs[ti % 2]
        XT_ALL = XTN[ti % 2]
        for ec in ecs:
            half, el = ec // 4, ec % 4
            uzb = 8 + half
            ensure_loaded(ti * NUNITS + uzb + 2)
            wz, nz = wslot(ti, uzb)
            if True:
                g, dd = ec // 2, ec % 2
                sl = ec % 2
                b = next_bank()
                mm_group(b, lambda k, el=el: wz[:, k, el * 128:(el + 1) * 128], lambda k: xT[:, k, 8:520], 8, XT_ALL + [nz])
                S.op("act", lambda e, b=b, sl=sl, ec=ec: e.activation(
                    out=szb[sl][:], in_=ps[:, b, :], func=AF.Silu, bias=ppar[:, CB_ZB + ec:CB_ZB + ec + 1], scale=1.0),
                    reads=[f"ps{b}", "ppar"], writes=[f"szb{sl}"])
                b2 = next_bank()
                mm_group(b2, lambda k, g=g, dd=dd: wpool[:, g * 2 + k, dd * 128:(dd + 1) * 128],
                         lambda k, g=g: pooled[:, 2 * g + k, :], 2, ["wpool", f"pooled{2 * g}", f"pooled{2 * g + 1}"])
                S.op("dve", lambda e, b2=b2, sl=sl, ec=ec: e.scalar_tensor_tensor(
                    out=szb[sl][:], in0=ps[:, b2, :], scalar=ppar[:, 56 + ec:57 + ec], in1=szb[sl][:],
                    op0=ALU.add, op1=ALU.mult), reads=[f"ps{b2}", "ppar", f"szb{sl}"], writes=[f"szb{sl}"])
                S.op("act", lambda e, sl=sl, ec=ec: e.activation(
                    out=brb[:, ec, :], in_=szb[sl][:], func=AF.Identity, bias=0.0, scale=ppar[:, 64 + ec:65 + ec]),
                    reads=[f"szb{sl}", "ppar"], writes=[f"brb{ec}"])

    BRA_ALL = [f"bra{g}" for g in range(8)]
    BRB_ALL = [f"brb{g}" for g in range(8)]

    def stage_merge(ti):
        xT = xTs[ti % 2]
        XT_ALL = XTN[ti % 2]
        for q in range(4):
            ug = 10 + 2 * q
            ub = 11 + 2 * q
            ensure_loaded(ti * NUNITS + ub + 2)
            wg, ng = wslot(ti, ug)
            wb, nb = wslot(ti, ub)
            for dl in range(2):
                dd = 2 * q + dl
                sl = dd % 2
                b = next_bank()
                mm_group(b, lambda k, dl=dl: wg[:, k, dl * 128:(dl + 1) * 128], lambda k: xT[:, k, 8:520], 8, XT_ALL + [ng])
                S.op("act", lambda e, b=b, sl=sl, dd=dd: e.activation(
                    out=sga[sl][:], in_=ps[:, b, :], func=AF.Sigmoid, bias=ppar[:, CB_GA + dd:CB_GA + dd + 1], scale=1.0),
                    reads=[f"ps{b}", "ppar"], writes=[f"sga{sl}"])
                b = next_bank()
                mm_group(b, lambda k, dl=dl: wg[:, k, 256 + dl * 128:256 + (dl + 1) * 128], lambda k: xT[:, k, 8:520], 8,
                         XT_ALL + [ng])
                S.op("act", lambda e, b=b, sl=sl, dd=dd: e.activation(
                    out=sgb[sl][:], in_=ps[:, b, :], func=AF.Sigmoid, bias=ppar[:, CB_GB + dd:CB_GB + dd + 1], scale=1.0),
                    reads=[f"ps{b}", "ppar"], writes=[f"sgb{sl}"])
                b = next_bank()
                mm_group(b, lambda k, dl=dl: wb[:, k, dl * 128:(dl + 1) * 128], lambda k: bra[:, k, :], 8, BRA_ALL + [nb])
                S.op("dve", lambda e, b=b, sl=sl: e.tensor_tensor(out=sga[sl][:], in0=sga[sl][:], in1=ps[:, b, :], op=ALU.mult),
                     reads=[f"ps{b}", f"sga{sl}"], writes=[f"sga{sl}"])
                b = next_bank()
                mm_group(b, lambda k, dl=dl: wb[:, k, 256 + dl * 128:256 + (dl + 1) * 128], lambda k: brb[:, k, :], 8,
                         BRB_ALL + [nb])
                S.op("dve", lambda e, b=b, sl=sl: e.tensor_tensor(out=sgb[sl][:], in0=sgb[sl][:], in1=ps[:, b, :], op=ALU.mult),
                     reads=[f"ps{b}", f"sgb{sl}"], writes=[f"sgb{sl}"])
                S.op("pool", lambda e, sl=sl, dd=dd: e.tensor_tensor(out=merged[:, dd, :], in0=sga[sl][:], in1=sgb[sl][:], op=ALU.add),
                     reads=[f"sga{sl}", f"sgb{sl}"], writes=[f"merged{dd}"])

    MERGED_ALL = [f"merged{d}" for d in range(8)]

    def stage_xf_dma(ti):
        seg, t = tiles[ti]
        base = SEG_PAD_OFF[seg] + T * t + 8
        for j in range(NCH):
            S.dma("sp", xf[j][:], xpad[base + 128 * j:base + 128 * j + 128, :], f"xfld{j}", writes=[f"xf{j}h0", f"xf{j}h1"])

    def stage_xf_load(ti):
        for j in range(NCH):
            S.op("pool", lambda e, j=j: e.tensor_tensor(out=xf[j][:], in0=xf[j][:], in1=cst[:, 2:3].broadcast_to([128, D]), op=ALU.mult),
                 reads=[f"xf{j}h0", f"xf{j}h1", "cst"], writes=[f"xf{j}h0", f"xf{j}h1"])
            S.op("pool", lambda e, j=j: e.tensor_tensor(out=xf[j][:], in0=xf[j][:], in1=tpar[:, 3, :], op=ALU.add),
                 reads=[f"xf{j}h0", f"xf{j}h1", "tpar"], writes=[f"xf{j}h0", f"xf{j}h1"])

    def stage_out(ti):
        seg, t = tiles[ti]
        ensure_loaded(ti * NUNITS + 19 + 2)
        w0, n0 = wslot(ti, 18)
        w1, n1 = wslot(ti, 19)
        obase = SEG_OUT_OFF[seg] + T * t
        for j in range(NCH):
            sl = 4 + j
            mvs = mv[:, sl, :]
            names = [f"xf{j}h0", f"xf{j}h1"]

            def part1(j=j, sl=sl, mvs=mvs):
                hw = ((w0, n0), (w1, n1))
                bks = [next_bank(), next_bank()]
                for ks in (range(0, 7), range(7, 8)):
                    for h, (w, wn) in enumerate(hw):
                        for k in ks:
                            S.op("pe", lambda e, k=k, w=w, h=h: e.matmul(
                                ps[:, bks[h], :], lhsT=merged[:, k, 128 * j:128 * j + 128], rhs=w[:, k, :],
                                start=(k == 0), stop=(k == 7)),
                                reads=[f"merged{k}", wn], writes=[f"ps{bks[h]}"], inc=(k == 7))
                for h, (w, wn) in enumerate(hw):
                    b = bks[h]
                    S.op("dve", lambda e, h=h, b=b: e.tensor_tensor(
                        out=xf[j][:, h * 512:(h + 1) * 512], in0=xf[j][:, h * 512:(h + 1) * 512], in1=ps[:, b, :],
                        op=ALU.add), reads=[f"ps{b}", f"xf{j}h{h}"], writes=[f"xf{j}h{h}"])
                    S.op("dve", lambda e, h=h: e.bn_stats(out=stats[:, sl, h, :], in_=xf[j][:, h * 512:(h + 1) * 512]),
                         reads=[f"xf{j}h{h}"], writes=[f"stats{sl}"] if h == 1 else [f"stats{sl}a"])
                S.op("dve", lambda e: e.bn_aggr(out=mvs[:, 0:2], in_=stats[:, sl, :, :]),
                     reads=[f"stats{sl}", f"stats{sl}a"], writes=[f"mv{sl}"])
                S.op("pool", lambda e: e.tensor_tensor(out=mvs[:, 3:4], in0=mvs[:, 1:2], in1=cst[:, 0:1], op=ALU.add),
                     reads=[f"mv{sl}", "cst"], writes=[f"mvt{sl}"])
                S.op("pool", lambda e: e.tensor_tensor(out=mvs[:, 2:3], in0=mvs[:, 3:4], in1=cst[:, 1:2], op=ALU.pow),
                     reads=[f"mvt{sl}", "cst"], writes=[f"mvr{sl}"])

            def part2(j=j, sl=sl, mvs=mvs, names=names):
                S.op("dve", lambda e: e.scalar_tensor_tensor(out=mvs[:, 3:4], in0=mvs[:, 0:1], scalar=-1.0,
                                                             in1=mvs[:, 2:3], op0=ALU.mult, op1=ALU.mult),
                     reads=[f"mv{sl}", f"mvr{sl}"], writes=[f"mvt{sl}"])
                S.op("act", lambda e: e.activation(
                    out=xf[j][:], in_=xf[j][:], func=AF.Identity, bias=mvs[:, 3:4], scale=mvs[:, 2:3]),
                    reads=names + [f"mvt{sl}", f"mvr{sl}"], writes=names)
                S.op("dve", lambda e: e.tensor_tensor(out=xf[j][:], in0=xf[j][:], in1=tpar[:, 4, :], op=ALU.mult),
                     reads=names + ["tpar"], writes=names)
                S.op("pool", lambda e: e.tensor_tensor(out=xf[j][:], in0=xf[j][:], in1=tpar[:, 5, :], op=ALU.add),
                     reads=names + ["tpar"], writes=names)
                S.dma("act", y_d[obase + 128 * j:obase + 128 * j + 128, :], xf[j][:], f"yst{j}", reads=names)

            chunk_step(part1, part2)

    ntiles = len(tiles)
    S.prewait("pool", reads=["xbf0h0", "xbf0h1"])
    ensure_loaded(1)
    S.prewait("pool", reads=XTN[0])
    ensure_loaded(3)
    S.dma("pool", wspT[:], wspT_d, "c_wsp", writes=["wspT"])
    S.dma("pool", pa1[:], pa1_d, "c_pa1", writes=["pa1"])
    S.dma("pool", pa2[:], pa2_d, "c_pa2", writes=["pa2"])
    S.dma("pool", wpool[:], wpool_d, "c_wpool", writes=["wpool"])
    for ti in range(ntiles):
        if ti + 1 < ntiles:
            cast_x(ti + 1)
        stage_p(ti)
        stage_v(ti)
        stage_xf_dma(ti)
        stage_zu(ti)
        if ti + 1 < ntiles:
            transpose_x(ti + 1)
        for r in range(4):
            stage_spatial(ti, (2 * r, 2 * r + 1))
            stage_pool(ti, (2 * r, 2 * r + 1))
            stage_zb_m(ti, (2 * r, 2 * r + 1))
            if r == 0:
                stage_xf_load(ti)
        stage_merge(ti)
        stage_out(ti)
    run_deferred()
    S.wait_all_dma("act", [f"yst{j}" for j in range(NCH)])
    return nc


def _pool_mats(c0, L):
    a1 = np.zeros((4, 128, 128), np.float64)
    a2 = np.zeros((4, 16, 16), np.float64)
    for g, w in enumerate(POOL_WINDOWS):
        for t in range(128):
            tt = c0 + t
            lo = min(max(tt - w // 2, 0), L)
            hi = min(max(tt + w - w // 2, 0), L)
            cnt = hi - lo
            for st in range(lo, hi):
                _add(a1, a2, g, st, t, c0, 1.0 / cnt)
            _add(a1, a2, g, tt, t, c0, -1.0)
    return a1, a2


def _add(a1, a2, g, st, t, c0, val):
    s1 = st - (c0 - 8)
    if 0 <= s1 < 128:
        a1[g, s1, t] += val
    else:
        s2 = st - (c0 + 120)
        assert 0 <= s2 < 16 and t >= 112, (st, t, c0)
        a2[g, s2, t - 112] += val


def _hilo(a):
    a = a.astype(np.float32)
    hi = a.astype(ml_dtypes.bfloat16).astype(np.float32)
    lo = (a - hi).astype(ml_dtypes.bfloat16).astype(np.float32)
    return hi, lo


_NC_CACHE = {}


def kernel(x_prompt, x_sample, w_in, b_in, ln_v_g, ln_v_b, w_spatial, b_spatial, w_pool, b_pool,
           pool_scale, w_br_a, w_br_b, w_out, b_out, ln_g, ln_b):
    f32 = np.float32
    x_prompt = np.asarray(x_prompt, f32)
    x_sample = np.asarray(x_sample, f32)
    w_in = np.asarray(w_in, f32)[0]
    b_in = np.asarray(b_in, f32)[0]
    w_br_a = np.asarray(w_br_a, f32)[0]
    w_br_b = np.asarray(w_br_b, f32)[0]
    w_out = np.asarray(w_out, f32)[0]

    def unit_from(cols_list):
        m = np.concatenate(cols_list, axis=1)
        return m.reshape(8, 128, 512).transpose(1, 0, 2)

    units = []
    units += [unit_from([w_in[:, 3072 + 512 * h:3072 + 512 * (h + 1)]]) for h in range(2)]
    units += [unit_from([w_in[:, 1024 + 512 * h:1024 + 512 * (h + 1)]]) for h in range(2)]
    for h in range(2):
        units.append(unit_from([w_in[:, 2048 + 512 * h:2048 + 512 * (h + 1)]]))
        units.append(unit_from([w_in[:, 0 + 512 * h:512 * (h + 1)]]))
    units += [unit_from([w_in[:, 4096 + 512 * h:4096 + 512 * (h + 1)]]) for h in range(2)]
    for q in range(4):
        units.append(unit_from([w_in[:, 5120 + 256 * q:5120 + 256 * (q + 1)],
                                w_in[:, 6144 + 256 * q:6144 + 256 * (q + 1)]]))
        units.append(unit_from([w_br_a[:, 256 * q:256 * (q + 1)], w_br_b[:, 256 * q:256 * (q + 1)]]))
    units += [unit_from([w_out[:, 512 * h:512 * (h + 1)]]) for h in range(2)]
    wall = np.ascontiguousarray(np.stack(units, axis=0), dtype=f32)

    ppar = np.zeros((128, 72), f32)
    ppar[:, 0:56] = b_in.reshape(56, 128).T
    ppar[:, 56:64] = np.asarray(b_pool, f32).reshape(8, 128).T
    ppar[:, 64:72] = np.asarray(pool_scale, f32).reshape(8, 128).T
    tpar = np.stack([b_in[1024:2048], b_in[3072:4096], np.asarray(ln_v_g, f32)[0], np.asarray(ln_v_b, f32)[0],
                     np.asarray(b_out, f32)[0], np.asarray(ln_g, f32)[0], np.asarray(ln_b, f32)[0]], axis=0).astype(f32)
    bsp = np.asarray(b_spatial, f32).reshape(1, 1024)
    wspT = np.ascontiguousarray(np.asarray(w_spatial, f32)[0].transpose(2, 0, 1))
    wp = np.asarray(w_pool, f32)[0]
    wpool = np.ascontiguousarray(wp.reshape(4, 2, 128, 256).transpose(2, 0, 1, 3).reshape(128, 8, 256))

    in_maps = []
    for c in range(NCORES):
        b, qd = c // 4, c % 4
        xp = np.zeros((XPAD_ROWS, D), f32)
        lo, hi = qd * 4096 - 8, qd * 4096 + 4096 + 8
        slo, shi = max(lo, 0), min(hi, 16384)
        xp[slo - lo:slo - lo + (shi - slo)] = x_prompt[b, slo:shi]
        for k in range(2):
            off = SEG_PAD_OFF[1 + k] + 8
            xp[off:off + 2048] = x_sample[2 * c + k]
        specs = [(1024, 16384), (qd * 4096, 16384), (qd * 4096 + 4096 - 128, 16384), (0, 2048), (2048 - 128, 2048)]
        pa1 = np.zeros((128, 40, 128), f32)
        pa2 = np.zeros((128, 40, 16), f32)
        for v, (c0, L) in enumerate(specs):
            a1, a2 = _pool_mats(c0, L)
            h1, l1 = _hilo(a1)
            h2, l2 = _hilo(a2)
            for g in range(4):
                pa1[:, (v * 2 + 0) * 4 + g, :] = h1[g]
                pa1[:, (v * 2 + 1) * 4 + g, :] = l1[g]
                pa2[0:16, (v * 2 + 0) * 4 + g, :] = h2[g]
                pa2[0:16, (v * 2 + 1) * 4 + g, :] = l2[g]
        in_maps.append({"xpad": xp, "wall": wall, "ppar": ppar, "tpar": tpar, "bsp": bsp, "wspT": wspT,
                        "wpool": wpool, "pa1": pa1, "pa2": pa2})

    if "nc" not in _NC_CACHE:
        _NC_CACHE["nc"] = build_nc()
    nc = _NC_CACHE["nc"]
    res = run_bass_kernel_spmd(nc, in_maps, core_ids=list(range(NCORES)))
    y_prompt = np.zeros((2, 16384, D), f32)
    y_sample = np.zeros((16, 2048, D), f32)
    for c in range(NCORES):
        y = np.asarray(res.results[c]["y"], f32)
        b, qd = c // 4, c % 4
        y_prompt[b, qd * 4096:(qd + 1) * 4096] = y[0:4096]
        y_sample[2 * c] = y[4096:6144]
        y_sample[2 * c + 1] = y[6144:8192]
    return (y_prompt, y_sample)
```

```python
import numpy as np
import ml_dtypes
import concourse.bass as bass
import concourse.mybir as mybir
from concourse.bass_utils import run_bass_kernel_spmd

F32 = mybir.dt.float32
BF16 = mybir.dt.bfloat16
AF = mybir.ActivationFunctionType
ALU = mybir.AluOpType

D = 1024
NCORES = 8
T = 512
NCH = 4
SEGS = (4096, 2048, 2048)
SEG_PAD_OFF = (0, 4112, 6176)
SEG_OUT_OFF = (0, 4096, 6144)
XPAD_ROWS = 8240
NTOK = 8192
NUNITS = 20
NSLOT = 4
ALPHA = float(2.0 ** 0.25)
LN_EPS = 1e-5
POOL_WINDOWS = (2, 4, 8, 16)
CB_U, CB_V, CB_ZA, CB_P, CB_ZB, CB_GA, CB_GB = 0, 8, 16, 24, 32, 40, 48


class Sched:
    def __init__(self, nc):
        self.nc = nc
        self.eng = {"pe": nc.tensor, "act": nc.scalar, "dve": nc.vector,
                    "pool": nc.gpsimd, "sp": nc.sync}
        self.sem = {}
        self.tick = {}
        for e in ("pe", "act", "dve", "pool"):
            self.sem[e] = nc.alloc_semaphore("s_" + e)
            self.tick[e] = 0
        self.waited = {e: {} for e in self.eng}
        self.last_write = {}
        self.readers = {}
        self.dma_sem = {}
        self.dma_cnt = {}

    def _deps(self, reads, writes):
        deps = {}

        def add(k, v):
            if k not in deps or deps[k] < v:
                deps[k] = v

        for b in reads:
            lw = self.last_write.get(b)
            if lw is not None:
                add(*lw)
            if b.startswith("ps"):
                for k, v in self.readers.get(b, {}).items():
                    add(k, v)
        for b in writes:
            lw = self.last_write.get(b)
            if lw is not None:
                add(*lw)
            for k, v in self.readers.get(b, {}).items():
                add(k, v)
        return deps

    def _emit_waits(self, engine, deps, embed=False):
        need = []
        for k, v in deps.items():
            if k == "pe" and engine == "pe":
                continue
            if k in self.tick and v > self.tick[k]:
                raise RuntimeError(f"dep on unclosed milestone {k} {v} > {self.tick[k]}")
            if self.waited[engine].get(k, 0) >= v:
                continue
            sem = self.sem[k] if k in self.sem else self.dma_sem[k]
            need.append((sem, v))
            self.waited[engine][k] = v
        last = need.pop() if (embed and need) else None
        for sem, v in need:
            self.eng[engine].wait_ge(sem, v)
        return last

    def _record(self, key, val, reads, writes):
        for b in writes:
            self.last_write[b] = (key, val)
            self.readers[b] = {}
        for b in reads:
            r = self.readers.setdefault(b, {})
            if r.get(key, 0) < val:
                r[key] = val

    def op(self, engine, fn, reads=(), writes=(), inc=True):
        w = self._emit_waits(engine, self._deps(reads, writes), embed=True)
        ins = fn(self.eng[engine])
        if w is not None:
            ins._wait_ge(w[0], w[1])
        if inc:
            self.tick[engine] += 1
            ins.then_inc(self.sem[engine], 1)
            val = self.tick[engine]
        else:
            assert engine == "pe"
            val = self.tick[engine] + 1
        self._record(engine, val, reads, writes)
        return ins

    def prewait(self, engine, reads=(), writes=()):
        self._emit_waits(engine, self._deps(reads, writes))

    def dma(self, queue, out, in_, slot, reads=(), writes=(), transpose=False):
        w = self._emit_waits(queue, self._deps(reads, writes), embed=True)
        if slot not in self.dma_sem:
            self.dma_sem[slot] = self.nc.alloc_semaphore("d_" + slot)
            self.dma_cnt[slot] = 0
        if transpose:
            ins = self.eng[queue].dma_start_transpose(out=out, in_=in_)
        else:
            ins = self.eng[queue].dma_start(out=out, in_=in_)
        if w is not None:
            ins._wait_ge(w[0], w[1])
        self.dma_cnt[slot] += 16
        ins.then_inc(self.dma_sem[slot], 16)
        self._record(slot, self.dma_cnt[slot], reads, writes)
        return ins

    def wait_all_dma(self, engine, slots):
        for s in slots:
            if self.dma_cnt.get(s, 0) > 0 and self.waited[engine].get(s, 0) < self.dma_cnt[s]:
                self.eng[engine].wait_ge(self.dma_sem[s], self.dma_cnt[s])
                self.waited[engine][s] = self.dma_cnt[s]


def build_nc():
    nc = bass.Bass("TRN2", target_bir_lowering=False)
    S = Sched(nc)

    xpad = nc.dram_tensor("xpad", [XPAD_ROWS, D], F32, kind="ExternalInput").ap()
    wall = nc.dram_tensor("wall", [NUNITS, 128, 8, 512], F32, kind="ExternalInput").ap()
    ppar_d = nc.dram_tensor("ppar", [128, 72], F32, kind="ExternalInput").ap()
    tpar_d = nc.dram_tensor("tpar", [7, D], F32, kind="ExternalInput").ap()
    bsp_d = nc.dram_tensor("bsp", [1, D], F32, kind="ExternalInput").ap()
    wspT_d = nc.dram_tensor("wspT", [128, 8, 128], F32, kind="ExternalInput").ap()
    wpool_d = nc.dram_tensor("wpool", [128, 8, 256], F32, kind="ExternalInput").ap()
    pa1_d = nc.dram_tensor("pa1", [128, 40, 128], F32, kind="ExternalInput").ap()
    pa2_d = nc.dram_tensor("pa2", [128, 40, 16], F32, kind="ExternalInput").ap()
    y_d = nc.dram_tensor("y", [NTOK, D], F32, kind="ExternalOutput").ap()
    wall_bf = nc.dram_tensor("wall_bf", [NUNITS, 128, 8, 512], BF16, kind="Internal").ap()
    xbf = nc.dram_tensor("xbf", [2, 640, D], BF16, kind="Internal").ap()

    A = nc.alloc_sbuf_tensor
    pa1 = A("pa1_sb", [128, 40, 128], BF16)
    pa2 = A("pa2_sb", [128, 40, 16], BF16)
    ppar = A("ppar_sb", [128, 72], F32)
    tpar = A("tpar_sb", [128, 6, D], F32)
    wspT = A("wspT_sb", [128, 8, 128], BF16)
    wpool = A("wpool_sb", [128, 8, 256], BF16)
    bsp_rows = A("bsp_rows", [128, D], BF16)
    bv_rows = A("bv_rows", [128, D], BF16)
    ones33 = A("ones33", [128, 128], BF16)
    cst = A("cst", [128, 3], F32)
    ws = [A(f"ws{i}", [128, 8, 512], BF16) for i in range(NSLOT)]
    xTs = [A(f"xT{i}", [128, 8, 640], BF16) for i in range(2)]
    xf = [A(f"xf{i}", [128, D], F32) for i in range(4)]
    v32 = [A(f"v32_{i}", [128, D], F32) for i in range(2)]
    vfull = A("vfull", [128, NCH, D], BF16)
    pbuf = A("pbuf", [128, 5, D], BF16)
    sza = [A(f"sza{i}", [128, T], F32) for i in range(2)]
    szb = [A(f"szb{i}", [128, T], F32) for i in range(2)]
    us = A("us", [128, 8, T], F32)
    bra = A("bra", [128, 8, T], BF16)
    pooled = A("pooled", [128, 8, T], BF16)
    brb = A("brb", [128, 8, T], BF16)
    merged = A("merged", [128, 8, T], BF16)
    sga = [A(f"sga{i}", [128, T], F32) for i in range(2)]
    sgb = [A(f"sgb{i}", [128, T], F32) for i in range(2)]
    stats = A("stats", [128, 8, 2, 6], F32)
    mv = A("mv", [128, 8, 4], F32)
    ps = nc.alloc_psum_tensor("ps", [128, 8, 512], F32)
    bank_ctr = [0]

    def next_bank():
        b = bank_ctr[0] % 8
        bank_ctr[0] += 1
        return b

    XTN = [[f"xT{par}_{kc}" for kc in range(8)] for par in range(2)]

    def seg_pos(ti):
        seg, t = tiles[ti]
        nt = SEGS[seg] // T
        return t, (t == 0), (t == nt - 1)

    def x_rows(ti):
        return 528 if seg_pos(ti)[2] else 640

    def pslot(ti, win):
        return (4 * seg_pos(ti)[0] + win) % 5

    def cast_x(ti):
        seg, t = tiles[ti]
        base = SEG_PAD_OFF[seg] + T * t
        par = ti % 2
        R = x_rows(ti)
        S.dma("pool", xbf[par][0:R, :], xpad[base:base + R, :], f"xcast{par}", writes=[f"xbf{par}"])

    def transpose_x(ti):
        par = ti % 2
        R = x_rows(ti)
        for kc in range(8):
            S.dma("sp", xTs[par][:, kc, 0:R], xbf[par][0:R, kc * 128:(kc + 1) * 128], f"xtr{par}",
                  reads=[f"xbf{par}"], writes=[XTN[par][kc]], transpose=True)
        for kc in range(8):
            S.last_write[XTN[par][kc]] = (f"xtr{par}", S.dma_cnt[f"xtr{par}"])

    tiles = []
    for s, L in enumerate(SEGS):
        for t in range(L // T):
            tiles.append((s, t))
    cast_x(0)
    S.dma("sp", ppar[:], ppar_d, "c_ppar", writes=["ppar"])
    S.dma("sp", tpar[:].rearrange("p k d -> p (k d)"),
          tpar_d[1:7, :].rearrange("k d -> (k d)").partition_broadcast(128), "c_tpar", writes=["tpar"])
    transpose_x(0)

    def make_hilo_rows(src_row_ap, rows_tile, rows_name, sem, b32, n32, bhi, nhi, bhi32, nhi32):
        S.op("dve", lambda e: e.memset(b32, 0.0), writes=n32)
        S.dma("act", b32[0:1, :], src_row_ap, sem, reads=n32, writes=n32)
        S.dma("act", b32[32:33, :], src_row_ap, sem, writes=n32)
        S.op("dve", lambda e: e.memset(rows_tile[:], 0.0), writes=[rows_name])
        S.op("dve", lambda e: e.tensor_copy(out=bhi, in_=b32), reads=n32, writes=nhi)
        S.op("dve", lambda e: e.tensor_copy(out=bhi32, in_=bhi), reads=nhi, writes=nhi32)
        S.op("dve", lambda e: e.tensor_sub(out=b32, in0=b32, in1=bhi32), reads=n32 + nhi32, writes=n32)
        S.op("dve", lambda e: e.tensor_copy(out=rows_tile[0:1, :], in_=bhi[0:1, :]),
             reads=nhi + [rows_name], writes=[rows_name])
        S.op("dve", lambda e: e.tensor_copy(out=rows_tile[32:33, :], in_=b32[32:33, :]),
             reads=n32 + [rows_name], writes=[rows_name])

    make_hilo_rows(tpar_d[0:1, :], bv_rows, "bv_rows", "c_bv",
                   v32[0][0:33, :], ["v32_0h0", "v32_0h1"], pbuf[0:33, 0, :], ["p0h0", "p0h1"],
                   xf[0][0:33, :], ["xf0h0", "xf0h1"])
    make_hilo_rows(bsp_d, bsp_rows, "bsp_rows", "c_bsp",
                   v32[1][0:33, :], ["v32_1h0", "v32_1h1"], pbuf[0:33, 1, :], ["p1h0", "p1h1"],
                   xf[1][0:33, :], ["xf1h0", "xf1h1"])
    S.op("dve", lambda e: e.memset(ones33[:], 1.0), writes=["ones33"])
    S.op("dve", lambda e: e.memset(pbuf[:], 0.0), writes=[f"p{i}h{h}" for i in range(5) for h in range(2)])
    S.op("pool", lambda e: e.memset(cst[:, 0:1], LN_EPS), writes=["cst"])
    S.op("pool", lambda e: e.memset(cst[:, 1:2], -0.5), writes=["cst"])
    S.op("pool", lambda e: e.memset(cst[:, 2:3], ALPHA), writes=["cst"])

    wstate = {"next": 0}

    def ensure_loaded(upto):
        while wstate["next"] <= upto and wstate["next"] < NUNITS * len(tiles):
            U = wstate["next"]
            u = U % NUNITS
            sl = U % NSLOT
            if U < NUNITS:
                S.dma("pool", ws[sl][:], wall[u], f"wldc{sl}", writes=[f"ws{sl}"])
                S.dma("sp", wall_bf[u], ws[sl][:], f"wwb{u}", reads=[f"ws{sl}"], writes=[f"wall{u}"])
            else:
                S.dma("sp", ws[sl][:], wall_bf[u], f"wld{sl}", reads=[f"wall{u}"], writes=[f"ws{sl}"])
            wstate["next"] += 1

    def wslot(ti, u):
        U = ti * NUNITS + u
        return ws[U % NSLOT], f"ws{U % NSLOT}"

    deferred = []

    def run_deferred():
        while deferred:
            deferred.pop(0)()

    def chunk_step(part1, part2):
        part1()
        run_deferred()
        deferred.append(part2)

    def mm_group(b, lhs_fn, rhs_fn, nk, reads, out_ap=None):
        o = ps[:, b, :] if out_ap is None else out_ap
        for k in range(nk):
            S.op("pe", lambda e, k=k: e.matmul(o, lhsT=lhs_fn(k), rhs=rhs_fn(k), start=(k == 0), stop=(k == nk - 1)),
                 reads=reads, writes=[f"ps{b}"], inc=(k == nk - 1))

    def stage_v(ti):
        xT = xTs[ti % 2]
        XT_ALL = XTN[ti % 2]
        ensure_loaded(ti * NUNITS + 5)
        w0, n0 = wslot(ti, 2)
        w1, n1 = wslot(ti, 3)
        for j in range(NCH):
            sl = j
            vs = j % 2
            mvs = mv[:, sl, :]
            banks = []
            src = v32[vs]
            names = [f"v32_{vs}h0", f"v32_{vs}h1"]

            def part1(j=j, sl=sl, mvs=mvs, banks=banks, src=src, names=names):
                for h, (w, wn) in enumerate(((w0, n0), (w1, n1))):
                    b = next_bank()
                    banks.append(b)
                    o = ps[:, b, :]
                    S.op("pe", lambda e, h=h, o=o: e.matmul(o, lhsT=ones33[:], rhs=bv_rows[:, h * 512:(h + 1) * 512],
                                                            start=True, stop=False),
                         reads=["ones33", "bv_rows"], writes=[f"ps{b}"], inc=False)
                    for k in range(8):
                        S.op("pe", lambda e, k=k, w=w, o=o: e.matmul(
                            o, lhsT=xT[:, k, 8 + 128 * j:8 + 128 * j + 128], rhs=w[:, k, :], start=False, stop=(k == 7)),
                            reads=XT_ALL + [wn], writes=[f"ps{b}"], inc=(k == 7))
                    S.op("dve", lambda e, h=h, b=b: e.bn_stats(out=stats[:, sl, h, :], in_=ps[:, b, :]),
                         reads=[f"ps{b}"], writes=[f"stats{sl}"] if h == 1 else [f"stats{sl}a"])
                    S.op("act", lambda e, h=h, b=b: e.copy(out=src[:, h * 512:(h + 1) * 512], in_=ps[:, b, :]),
                         reads=[f"ps{b}"], writes=[names[h]])
                S.op("dve", lambda e: e.bn_aggr(out=mvs[:, 0:2], in_=stats[:, sl, :, :]),
                     reads=[f"stats{sl}", f"stats{sl}a"], writes=[f"mv{sl}"])
                S.op("pool", lambda e: e.tensor_tensor(out=mvs[:, 3:4], in0=mvs[:, 1:2], in1=cst[:, 0:1], op=ALU.add),
                     reads=[f"mv{sl}", "cst"], writes=[f"mvt{sl}"])
                S.op("pool", lambda e: e.tensor_tensor(out=mvs[:, 2:3], in0=mvs[:, 3:4], in1=cst[:, 1:2], op=ALU.pow),
                     reads=[f"mvt{sl}", "cst"], writes=[f"mvr{sl}"])

            def part2(j=j, sl=sl, mvs=mvs, banks=banks, src=src, names=names):
                S.op("dve", lambda e: e.scalar_tensor_tensor(out=mvs[:, 3:4], in0=mvs[:, 0:1], scalar=-1.0,
                                                             in1=mvs[:, 2:3], op0=ALU.mult, op1=ALU.mult),
                     reads=[f"mv{sl}", f"mvr{sl}"], writes=[f"mvt{sl}"])
                S.op("act", lambda e: e.activation(out=src[:], in_=src[:], func=AF.Identity, bias=mvs[:, 3:4],
                                                   scale=mvs[:, 2:3]),
                     reads=names + [f"mvt{sl}", f"mvr{sl}"], writes=names)
                S.op("dve", lambda e: e.tensor_tensor(out=src[:], in0=src[:], in1=tpar[:, 1, :], op=ALU.mult),
                     reads=names + ["tpar"], writes=names)
                S.op("pool", lambda e: e.tensor_tensor(out=vfull[:, j, :], in0=src[:], in1=tpar[:, 2, :], op=ALU.add),
                     reads=names + ["tpar"], writes=[f"vfull{j}"])

            chunk_step(part1, part2)

    def stage_p(ti):
        xT = xTs[ti % 2]
        XT_ALL = XTN[ti % 2]
        ensure_loaded(ti * NUNITS + 3)
        w0, n0 = wslot(ti, 0)
        w1, n1 = wslot(ti, 1)
        t, first, last = seg_pos(ti)
        wins = ([0] if first else []) + [1, 2, 3, 4]
        for n_w, win in enumerate(wins):
            m = 16 if (win == 4 and last) else 128
            sl = pslot(ti, win)
            for h, (w, wn) in enumerate(((w0, n0), (w1, n1))):
                b = next_bank()
                mm_group(b, lambda k, win=win, m=m: xT[:, k, 128 * win:128 * win + m], lambda k, w=w: w[:, k, :], 8,
                         XT_ALL + [wn], out_ap=ps[0:m, b, :])
                S.op("dve", lambda e, h=h, b=b, sl=sl, m=m: e.tensor_tensor(
                    out=pbuf[0:m, sl, h * 512:(h + 1) * 512], in0=tpar[0:m, 0, h * 512:(h + 1) * 512],
                    in1=ps[0:m, b, :], op=ALU.add), reads=[f"ps{b}", "tpar"], writes=[f"p{sl}h{h}"])
            if n_w == 0:
                run_deferred()

    def stage_zu(ti):
        xT = xTs[ti % 2]
        XT_ALL = XTN[ti % 2]
        for half in range(2):
            uz = 4 + 2 * half
            uu = 5 + 2 * half
            ensure_loaded(ti * NUNITS + uu + 2)
            wz, nz = wslot(ti, uz)
            wu, nu = wslot(ti, uu)
            for el in range(4):
                ec = half * 4 + el
                sl = ec % 2
                b = next_bank()
                mm_group(b, lambda k, el=el: wz[:, k, el * 128:(el + 1) * 128], lambda k: xT[:, k, 8:520], 8, XT_ALL + [nz])
                S.op("act", lambda e, b=b, sl=sl, ec=ec: e.activation(
                    out=sza[sl][:], in_=ps[:, b, :], func=AF.Silu, bias=ppar[:, CB_ZA + ec:CB_ZA + ec + 1], scale=1.0),
                    reads=[f"ps{b}", "ppar"], writes=[f"sza{sl}"])
                b2 = next_bank()
                mm_group(b2, lambda k, el=el: wu[:, k, el * 128:(el + 1) * 128], lambda k: xT[:, k, 8:520], 8, XT_ALL + [nu])
                S.op("dve", lambda e, b2=b2, sl=sl, ec=ec: e.scalar_tensor_tensor(
                    out=us[:, ec, :], in0=ps[:, b2, :], scalar=ppar[:, CB_U + ec:CB_U + ec + 1], in1=sza[sl][:],
                    op0=ALU.add, op1=ALU.mult), reads=[f"ps{b2}", "ppar", f"sza{sl}"], writes=[f"us{ec}"])
                if ec == 0:
                    run_deferred()

    def stage_spatial(ti, gs=range(8)):
        for g in gs:
            b = next_bank()
            o = ps[:, b, :]
            S.op("pe", lambda e, g=g, o=o: e.matmul(
                o.rearrange("p (j q) -> p j q", j=4), lhsT=ones33[:],
                rhs=bsp_rows[:, g * 128:(g + 1) * 128].unsqueeze(1).broadcast_to([128, 4, 128]),
                start=True, stop=False), reads=["ones33", "bsp_rows"], writes=[f"ps{b}"], inc=False)
            for j in range(NCH):
                S.op("pe", lambda e, g=g, j=j, o=o: e.matmul(
                    o[:, 128 * j:128 * j + 128], lhsT=vfull[:, j, g * 128:(g + 1) * 128], rhs=wspT[:, g, :],
                    start=False, stop=(j == NCH - 1)), reads=[f"vfull{j}", "wspT"], writes=[f"ps{b}"],
                    inc=(j == NCH - 1))
            S.op("dve", lambda e, g=g, b=b: e.tensor_tensor(out=bra[:, g, :], in0=us[:, g, :], in1=ps[:, b, :], op=ALU.mult),
                 reads=[f"ps{b}", f"us{g}"], writes=[f"bra{g}"])

    def chunk_variant(seg, cidx):
        nch = SEGS[seg] // 128
        if cidx == 0:
            return 1 if seg == 0 else 3
        if cidx == nch - 1:
            return 2 if seg == 0 else 4
        return 0

    def stage_pool(ti, ccs=range(8)):
        seg, t = tiles[ti]
        for cc in ccs:
            g = cc // 2
            b = next_bank()
            o = ps[:, b, :]
            for j in range(NCH):
                var = chunk_variant(seg, 4 * t + j)
                ops = []
                i_hi = (var * 2 + 0) * 4 + g
                i_lo = (var * 2 + 1) * 4 + g
                oj = o[:, 128 * j:128 * j + 128]
                oj2 = o[:, 128 * j + 112:128 * j + 128]
                s1, s2 = pslot(ti, j), pslot(ti, j + 1)
                l1 = pbuf[:, s1, cc * 128:(cc + 1) * 128]
                l2 = pbuf[:, s2, cc * 128:(cc + 1) * 128]
                ops.append((oj, l1, pa1[:, i_hi, :]))
                if var != 0:
                    ops.append((oj, l1, pa1[:, i_lo, :]))
                ops.append((oj2, l2, pa2[:, i_hi, :]))
                if var != 0:
                    ops.append((oj2, l2, pa2[:, i_lo, :]))
                rd = [f"p{s1}h{cc // 4}", f"p{s2}h{cc // 4}", "pa1", "pa2"]
                for n, (oo, ll, rr) in enumerate(ops):
                    last = (n == len(ops) - 1)
                    S.op("pe", lambda e, oo=oo, ll=ll, rr=rr, n=n, last=last: e.matmul(
                        oo, lhsT=ll, rhs=rr, start=(n == 0), stop=last),
                        reads=rd, writes=[f"ps{b}"], inc=(last and j == NCH - 1))
            S.op("act", lambda e, b=b, cc=cc: e.copy(out=pooled[:, cc, :], in_=ps[:, b, :]),
                 reads=[f"ps{b}"], writes=[f"pooled{cc}"])

    def stage_zb_m(ti, ecs=range(8)):
        xT = xTs[ti % 2]
        XT_ALL = XTN[ti % 2]
        for ec in ecs:
            half, el = ec // 4, ec % 4
            uzb = 8 + half
            ensure_loaded(ti * NUNITS + uzb + 2)
            wz, nz = wslot(ti, uzb)
            if True:
                g, dd = ec // 2, ec % 2
                sl = ec % 2
                b = next_bank()
                mm_group(b, lambda k, el=el: wz[:, k, el * 128:(el + 1) * 128], lambda k: xT[:, k, 8:520], 8, XT_ALL + [nz])
                S.op("act", lambda e, b=b, sl=sl, ec=ec: e.activation(
                    out=szb[sl][:], in_=ps[:, b, :], func=AF.Silu, bias=ppar[:, CB_ZB + ec:CB_ZB + ec + 1], scale=1.0),
                    reads=[f"ps{b}", "ppar"], writes=[f"szb{sl}"])
                b2 = next_bank()
                mm_group(b2, lambda k, g=g, dd=dd: wpool[:, g * 2 + k, dd * 128:(dd + 1) * 128],
                         lambda k, g=g: pooled[:, 2 * g + k, :], 2, ["wpool", f"pooled{2 * g}", f"pooled{2 * g + 1}"])
                S.op("dve", lambda e, b2=b2, sl=sl, ec=ec: e.scalar_tensor_tensor(
                    out=szb[sl][:], in0=ps[:, b2, :], scalar=ppar[:, 56 + ec:57 + ec], in1=szb[sl][:],
                    op0=ALU.add, op1=ALU.mult), reads=[f"ps{b2}", "ppar", f"szb{sl}"], writes=[f"szb{sl}"])
                S.op("act", lambda e, sl=sl, ec=ec: e.activation(
                    out=brb[:, ec, :], in_=szb[sl][:], func=AF.Identity, bias=0.0, scale=ppar[:, 64 + ec:65 + ec]),
                    reads=[f"szb{sl}", "ppar"], writes=[f"brb{ec}"])

    BRA_ALL = [f"bra{g}" for g in range(8)]
    BRB_ALL = [f"brb{g}" for g in range(8)]

    def stage_merge(ti):
        xT = xTs[ti % 2]
        XT_ALL = XTN[ti % 2]
        for q in range(4):
            ug = 10 + 2 * q
            ub = 11 + 2 * q
            ensure_loaded(ti * NUNITS + ub + 2)
            wg, ng = wslot(ti, ug)
            wb, nb = wslot(ti, ub)
            for dl in range(2):
                dd = 2 * q + dl
                sl = dd % 2
                b = next_bank()
                mm_group(b, lambda k, dl=dl: wg[:, k, dl * 128:(dl + 1) * 128], lambda k: xT[:, k, 8:520], 8, XT_ALL + [ng])
                S.op("act", lambda e, b=b, sl=sl, dd=dd: e.activation(
                    out=sga[sl][:], in_=ps[:, b, :], func=AF.Sigmoid, bias=ppar[:, CB_GA + dd:CB_GA + dd + 1], scale=1.0),
                    reads=[f"ps{b}", "ppar"], writes=[f"sga{sl}"])
                b = next_bank()
                mm_group(b, lambda k, dl=dl: wg[:, k, 256 + dl * 128:256 + (dl + 1) * 128], lambda k: xT[:, k, 8:520], 8,
                         XT_ALL + [ng])
                S.op("act", lambda e, b=b, sl=sl, dd=dd: e.activation(
                    out=sgb[sl][:], in_=ps[:, b, :], func=AF.Sigmoid, bias=ppar[:, CB_GB + dd:CB_GB + dd + 1], scale=1.0),
                    reads=[f"ps{b}", "ppar"], writes=[f"sgb{sl}"])
                b = next_bank()
                mm_group(b, lambda k, dl=dl: wb[:, k, dl * 128:(dl + 1) * 128], lambda k: bra[:, k, :], 8, BRA_ALL + [nb])
                S.op("dve", lambda e, b=b, sl=sl: e.tensor_tensor(out=sga[sl][:], in0=sga[sl][:], in1=ps[:, b, :], op=ALU.mult),
                     reads=[f"ps{b}", f"sga{sl}"], writes=[f"sga{sl}"])
                b = next_bank()
                mm_group(b, lambda k, dl=dl: wb[:, k, 256 + dl * 128:256 + (dl + 1) * 128], lambda k: brb[:, k, :], 8,
                         BRB_ALL + [nb])
                S.op("dve", lambda e, b=b, sl=sl: e.tensor_tensor(out=sgb[sl][:], in0=sgb[sl][:], in1=ps[:, b, :], op=ALU.mult),
                     reads=[f"ps{b}", f"sgb{sl}"], writes=[f"sgb{sl}"])
                S.op("pool", lambda e, sl=sl, dd=dd: e.tensor_tensor(out=merged[:, dd, :], in0=sga[sl][:], in1=sgb[sl][:], op=ALU.add),
                     reads=[f"sga{sl}", f"sgb{sl}"], writes=[f"merged{dd}"])

    MERGED_ALL = [f"merged{d}" for d in range(8)]

    def stage_xf_dma(ti):
        seg, t = tiles[ti]
        base = SEG_PAD_OFF[seg] + T * t + 8
        for j in range(NCH):
            S.dma("sp", xf[j][:], xpad[base + 128 * j:base + 128 * j + 128, :], f"xfld{j}", writes=[f"xf{j}h0", f"xf{j}h1"])

    def stage_xf_load(ti):
        for j in range(NCH):
            S.op("pool", lambda e, j=j: e.tensor_tensor(out=xf[j][:], in0=xf[j][:], in1=cst[:, 2:3].broadcast_to([128, D]), op=ALU.mult),
                 reads=[f"xf{j}h0", f"xf{j}h1", "cst"], writes=[f"xf{j}h0", f"xf{j}h1"])
            S.op("pool", lambda e, j=j: e.tensor_tensor(out=xf[j][:], in0=xf[j][:], in1=tpar[:, 3, :], op=ALU.add),
                 reads=[f"xf{j}h0", f"xf{j}h1", "tpar"], writes=[f"xf{j}h0", f"xf{j}h1"])

    def stage_out(ti):
        seg, t = tiles[ti]
        ensure_loaded(ti * NUNITS + 19 + 2)
        w0, n0 = wslot(ti, 18)
        w1, n1 = wslot(ti, 19)
        obase = SEG_OUT_OFF[seg] + T * t
        for j in range(NCH):
            sl = 4 + j
            mvs = mv[:, sl, :]
            names = [f"xf{j}h0", f"xf{j}h1"]

            def part1(j=j, sl=sl, mvs=mvs):
                hw = ((w0, n0), (w1, n1))
                bks = [next_bank(), next_bank()]
                for ks in (range(0, 7), range(7, 8)):
                    for h, (w, wn) in enumerate(hw):
                        for k in ks:
                            S.op("pe", lambda e, k=k, w=w, h=h: e.matmul(
                                ps[:, bks[h], :], lhsT=merged[:, k, 128 * j:128 * j + 128], rhs=w[:, k, :],
                                start=(k == 0), stop=(k == 7)),
                                reads=[f"merged{k}", wn], writes=[f"ps{bks[h]}"], inc=(k == 7))
                for h, (w, wn) in enumerate(hw):
                    b = bks[h]
                    S.op("dve", lambda e, h=h, b=b: e.tensor_tensor(
                        out=xf[j][:, h * 512:(h + 1) * 512], in0=xf[j][:, h * 512:(h + 1) * 512], in1=ps[:, b, :],
                        op=ALU.add), reads=[f"ps{b}", f"xf{j}h{h}"], writes=[f"xf{j}h{h}"])
                    S.op("dve", lambda e, h=h: e.bn_stats(out=stats[:, sl, h, :], in_=xf[j][:, h * 512:(h + 1) * 512]),
                         reads=[f"xf{j}h{h}"], writes=[f"stats{sl}"] if h == 1 else [f"stats{sl}a"])
                S.op("dve", lambda e: e.bn_aggr(out=mvs[:, 0:2], in_=stats[:, sl, :, :]),
                     reads=[f"stats{sl}", f"stats{sl}a"], writes=[f"mv{sl}"])
                S.op("pool", lambda e: e.tensor_tensor(out=mvs[:, 3:4], in0=mvs[:, 1:2], in1=cst[:, 0:1], op=ALU.add),
                     reads=[f"mv{sl}", "cst"], writes=[f"mvt{sl}"])
                S.op("pool", lambda e: e.tensor_tensor(out=mvs[:, 2:3], in0=mvs[:, 3:4], in1=cst[:, 1:2], op=ALU.pow),
                     reads=[f"mvt{sl}", "cst"], writes=[f"mvr{sl}"])

            def part2(j=j, sl=sl, mvs=mvs, names=names):
                S.op("dve", lambda e: e.scalar_tensor_tensor(out=mvs[:, 3:4], in0=mvs[:, 0:1], scalar=-1.0,
                                                             in1=mvs[:, 2:3], op0=ALU.mult, op1=ALU.mult),
                     reads=[f"mv{sl}", f"mvr{sl}"], writes=[f"mvt{sl}"])
                S.op("act", lambda e: e.activation(
                    out=xf[j][:], in_=xf[j][:], func=AF.Identity, bias=mvs[:, 3:4], scale=mvs[:, 2:3]),
                    reads=names + [f"mvt{sl}", f"mvr{sl}"], writes=names)
                S.op("dve", lambda e: e.tensor_tensor(out=xf[j][:], in0=xf[j][:], in1=tpar[:, 4, :], op=ALU.mult),
                     reads=names + ["tpar"], writes=names)
                final = (ti == len(tiles) - 1)
                S.op("dve" if final else "pool",
                     lambda e: e.tensor_tensor(out=xf[j][:], in0=xf[j][:], in1=tpar[:, 5, :], op=ALU.add),
                     reads=names + ["tpar"], writes=names)
                S.dma("sp" if final else "act", y_d[obase + 128 * j:obase + 128 * j + 128, :], xf[j][:], f"yst{j}",
                      reads=names)

            chunk_step(part1, part2)

    ntiles = len(tiles)
    S.prewait("pool", reads=["xbf0"])
    ensure_loaded(1)
    S.prewait("pool", reads=XTN[0])
    ensure_loaded(3)
    S.dma("pool", wspT[:], wspT_d, "c_wsp", writes=["wspT"])
    S.dma("pool", pa1[:], pa1_d, "c_pa1", writes=["pa1"])
    S.dma("pool", pa2[:], pa2_d, "c_pa2", writes=["pa2"])
    S.dma("pool", wpool[:], wpool_d, "c_wpool", writes=["wpool"])
    for ti in range(ntiles):
        if ti + 1 < ntiles:
            cast_x(ti + 1)
        stage_p(ti)
        stage_v(ti)
        stage_xf_dma(ti)
        stage_zu(ti)
        if ti + 1 < ntiles:
            transpose_x(ti + 1)
        for r in range(4):
            stage_spatial(ti, (2 * r, 2 * r + 1))
            stage_pool(ti, (2 * r, 2 * r + 1))
            stage_zb_m(ti, (2 * r, 2 * r + 1))
            if r == 0:
                stage_xf_load(ti)
        stage_merge(ti)
        stage_out(ti)
    run_deferred()
    S.wait_all_dma("act", [f"yst{j}" for j in range(NCH)])
    return nc


def _pool_mats(c0, L):
    a1 = np.zeros((4, 128, 128), np.float64)
    a2 = np.zeros((4, 16, 16), np.float64)
    for g, w in enumerate(POOL_WINDOWS):
        for t in range(128):
            tt = c0 + t
            lo = min(max(tt - w // 2, 0), L)
            hi = min(max(tt + w - w // 2, 0), L)
            cnt = hi - lo
            for st in range(lo, hi):
                _add(a1, a2, g, st, t, c0, 1.0 / cnt)
            _add(a1, a2, g, tt, t, c0, -1.0)
    return a1, a2


def _add(a1, a2, g, st, t, c0, val):
    s1 = st - (c0 - 8)
    if 0 <= s1 < 128:
        a1[g, s1, t] += val
    else:
        s2 = st - (c0 + 120)
        assert 0 <= s2 < 16 and t >= 112, (st, t, c0)
        a2[g, s2, t - 112] += val


def _hilo(a):
    a = a.astype(np.float32)
    hi = a.astype(ml_dtypes.bfloat16).astype(np.float32)
    lo = (a - hi).astype(ml_dtypes.bfloat16).astype(np.float32)
    return hi, lo


_NC_CACHE = {}


def kernel(x_prompt, x_sample, w_in, b_in, ln_v_g, ln_v_b, w_spatial, b_spatial, w_pool, b_pool,
           pool_scale, w_br_a, w_br_b, w_out, b_out, ln_g, ln_b):
    f32 = np.float32
    x_prompt = np.asarray(x_prompt, f32)
    x_sample = np.asarray(x_sample, f32)
    w_in = np.asarray(w_in, f32)[0]
    b_in = np.asarray(b_in, f32)[0]
    w_br_a = np.asarray(w_br_a, f32)[0]
    w_br_b = np.asarray(w_br_b, f32)[0]
    w_out = np.asarray(w_out, f32)[0]

    def unit_from(cols_list):
        m = np.concatenate(cols_list, axis=1)
        return m.reshape(8, 128, 512).transpose(1, 0, 2)

    units = []
    units += [unit_from([w_in[:, 3072 + 512 * h:3072 + 512 * (h + 1)]]) for h in range(2)]
    units += [unit_from([w_in[:, 1024 + 512 * h:1024 + 512 * (h + 1)]]) for h in range(2)]
    for h in range(2):
        units.append(unit_from([w_in[:, 2048 + 512 * h:2048 + 512 * (h + 1)]]))
        units.append(unit_from([w_in[:, 0 + 512 * h:512 * (h + 1)]]))
    units += [unit_from([w_in[:, 4096 + 512 * h:4096 + 512 * (h + 1)]]) for h in range(2)]
    for q in range(4):
        units.append(unit_from([w_in[:, 5120 + 256 * q:5120 + 256 * (q + 1)],
                                w_in[:, 6144 + 256 * q:6144 + 256 * (q + 1)]]))
        units.append(unit_from([w_br_a[:, 256 * q:256 * (q + 1)], w_br_b[:, 256 * q:256 * (q + 1)]]))
    units += [unit_from([w_out[:, 512 * h:512 * (h + 1)]]) for h in range(2)]
    wall = np.ascontiguousarray(np.stack(units, axis=0), dtype=f32)

    ppar = np.zeros((128, 72), f32)
    ppar[:, 0:56] = b_in.reshape(56, 128).T
    ppar[:, 56:64] = np.asarray(b_pool, f32).reshape(8, 128).T
    ppar[:, 64:72] = np.asarray(pool_scale, f32).reshape(8, 128).T
    tpar = np.stack([b_in[1024:2048], b_in[3072:4096], np.asarray(ln_v_g, f32)[0], np.asarray(ln_v_b, f32)[0],
                     np.asarray(b_out, f32)[0], np.asarray(ln_g, f32)[0], np.asarray(ln_b, f32)[0]], axis=0).astype(f32)
    bsp = np.asarray(b_spatial, f32).reshape(1, 1024)
    wspT = np.ascontiguousarray(np.asarray(w_spatial, f32)[0].transpose(2, 0, 1))
    wp = np.asarray(w_pool, f32)[0]
    wpool = np.ascontiguousarray(wp.reshape(4, 2, 128, 256).transpose(2, 0, 1, 3).reshape(128, 8, 256))

    in_maps = []
    for c in range(NCORES):
        b, qd = c // 4, c % 4
        xp = np.zeros((XPAD_ROWS, D), f32)
        lo, hi = qd * 4096 - 8, qd * 4096 + 4096 + 8
        slo, shi = max(lo, 0), min(hi, 16384)
        xp[slo - lo:slo - lo + (shi - slo)] = x_prompt[b, slo:shi]
        for k in range(2):
            off = SEG_PAD_OFF[1 + k] + 8
            xp[off:off + 2048] = x_sample[2 * c + k]
        specs = [(1024, 16384), (qd * 4096, 16384), (qd * 4096 + 4096 - 128, 16384), (0, 2048), (2048 - 128, 2048)]
        pa1 = np.zeros((128, 40, 128), f32)
        pa2 = np.zeros((128, 40, 16), f32)
        for v, (c0, L) in enumerate(specs):
            a1, a2 = _pool_mats(c0, L)
            h1, l1 = _hilo(a1)
            h2, l2 = _hilo(a2)
            for g in range(4):
                pa1[:, (v * 2 + 0) * 4 + g, :] = h1[g]
                pa1[:, (v * 2 + 1) * 4 + g, :] = l1[g]
                pa2[0:16, (v * 2 + 0) * 4 + g, :] = h2[g]
                pa2[0:16, (v * 2 + 1) * 4 + g, :] = l2[g]
        in_maps.append({"xpad": xp, "wall": wall, "ppar": ppar, "tpar": tpar, "bsp": bsp, "wspT": wspT,
                        "wpool": wpool, "pa1": pa1, "pa2": pa2})

    if "nc" not in _NC_CACHE:
        _NC_CACHE["nc"] = build_nc()
    nc = _NC_CACHE["nc"]
    res = run_bass_kernel_spmd(nc, in_maps, core_ids=list(range(NCORES)))
    y_prompt = np.zeros((2, 16384, D), f32)
    y_sample = np.zeros((16, 2048, D), f32)
    for c in range(NCORES):
        y = np.asarray(res.results[c]["y"], f32)
        b, qd = c // 4, c % 4
        y_prompt[b, qd * 4096:(qd + 1) * 4096] = y[0:4096]
        y_sample[2 * c] = y[4096:6144]
        y_sample[2 * c + 1] = y[6144:8192]
    return (y_prompt, y_sample)
```
